# Optimizing a Trainium2 kernel written in Bass

```python
import jax, jax.numpy as jnp
from jax import lax
import numpy as np

D_MODEL = 2048
BATCH = 4
SEQ = 2048
DEPTH = 1
DEC_BATCH = 128
DEC_SEQ = 4
PAST_LEN = 16384
PAGE_SIZE = 128

D_CONV = D_MODEL
CONV_WIDTH = 3
POOL_WINDOWS = (2, 4, 8, 16)
N_POOL_GROUPS = len(POOL_WINDOWS)
D_POOL = D_MODEL // 2
D_POOL_GROUP = D_POOL // N_POOL_GROUPS
POOL_BUF = max(POOL_WINDOWS) - 1
D_FF = ((8 * D_MODEL + 3 * 256 - 1) // (3 * 256)) * 256
D_IN = 3 * D_CONV + D_POOL + 2 * D_MODEL
EPS = 1e-6

kernel_name = "gated_shortconv_multiscale_pool_hybrid_step"


def rmsnorm(x, g):
    xf = x.astype(jnp.float32)
    r = lax.rsqrt(jnp.mean(xf * xf, axis=-1, keepdims=True) + EPS)
    return (xf * r).astype(x.dtype) * g


def pool_counts(pos0, T):
    t = jnp.arange(T, dtype=jnp.int32) + pos0
    w = jnp.array(POOL_WINDOWS, dtype=jnp.int32)
    return jnp.minimum(t[:, None] + 1, w[None, :]).astype(jnp.float32)


def causal_multiscale_pool(v_ext, counts):
    T = v_ext.shape[1] - POOL_BUF
    vf = v_ext.astype(jnp.float32)
    cs = jnp.cumsum(vf, axis=1)
    cs0 = jnp.concatenate([jnp.zeros_like(cs[:, :1]), cs], axis=1)
    end = cs0[:, POOL_BUF + 1:]
    outs = []
    for g, w in enumerate(POOL_WINDOWS):
        lo, hi = g * D_POOL_GROUP, (g + 1) * D_POOL_GROUP
        start = cs0[:, POOL_BUF + 1 - w: POOL_BUF + 1 - w + T, lo:hi]
        outs.append((end[..., lo:hi] - start) / counts[None, :, g:g + 1])
    mean = jnp.concatenate(outs, axis=-1)
    return (mean - vf[:, POOL_BUF:]).astype(v_ext.dtype)


def decoder_layer(x, conv_buf, pool_buf, counts, norm_mix, w_in, conv_w, w_pool, pool_scale,
                  w_br_conv, w_br_pool, w_o, norm_ffn, w_gate, w_up, w_down):
    Bsz, T, _ = x.shape
    xn = rmsnorm(x, norm_mix)
    proj = jnp.einsum('btd,de->bte', xn, w_in)
    h, b, c, v, gc, gp = jnp.split(
        proj, [D_CONV, 2 * D_CONV, 3 * D_CONV, 3 * D_CONV + D_POOL, 3 * D_CONV + D_POOL + D_MODEL], axis=-1)
    u_ext = jnp.concatenate([conv_buf, c * h], axis=1)
    conv = conv_w[0] * u_ext[:, 0:T] + conv_w[1] * u_ext[:, 1:T + 1] + conv_w[2] * u_ext[:, 2:T + 2]
    y_conv = jnp.einsum('btc,cd->btd', b * conv, w_br_conv)
    v_ext = jnp.concatenate([pool_buf, v], axis=1)
    pooled = causal_multiscale_pool(v_ext, counts).reshape(Bsz, T, N_POOL_GROUPS, D_POOL_GROUP)
    mixed = jnp.einsum('btgc,gcd->btgd', pooled, w_pool).reshape(Bsz, T, D_POOL) * pool_scale
    y_pool = jnp.einsum('btc,cd->btd', mixed, w_br_pool)
    merged = jax.nn.sigmoid(gc) * y_conv + jax.nn.sigmoid(gp) * y_pool
    x = x + jnp.einsum('btd,de->bte', merged, w_o)
    hn = rmsnorm(x, norm_ffn)
    ff = jax.nn.silu(jnp.einsum('btd,df->btf', hn, w_gate)) * jnp.einsum('btd,df->btf', hn, w_up)
    x = x + jnp.einsum('btf,fd->btd', ff, w_down)
    return x, u_ext[:, -(CONV_WIDTH - 1):], v_ext[:, -POOL_BUF:]


def setup_inputs(seed: int = 0) -> dict:
    key = jax.random.key(seed)
    ks = jax.random.split(key, 20)
    f32 = jnp.float32
    nrm = lambda k, s, sc: jax.random.normal(k, s, f32) * sc
    return {
        "x_prompt": nrm(ks[0], (BATCH, SEQ, D_MODEL), 1.0),
        "x_sample": nrm(ks[1], (DEC_BATCH, DEC_SEQ, D_MODEL), 1.0),
        "state_conv": nrm(ks[2], (DEPTH, DEC_BATCH, CONV_WIDTH - 1, D_CONV), 1.0),
        "state_pool": nrm(ks[3], (DEPTH, DEC_BATCH, POOL_BUF, D_POOL), 1.0),
        "norm_mix": 1.0 + nrm(ks[4], (DEPTH, D_MODEL), 0.1),
        "w_in": nrm(ks[5], (DEPTH, D_MODEL, D_IN), D_MODEL ** -0.5),
        "conv_w": nrm(ks[6], (DEPTH, CONV_WIDTH, D_CONV), CONV_WIDTH ** -0.5),
        "w_pool": nrm(ks[7], (DEPTH, N_POOL_GROUPS, D_POOL_GROUP, D_POOL_GROUP), D_POOL_GROUP ** -0.5),
        "pool_scale": 1.0 + nrm(ks[8], (DEPTH, D_POOL), 0.1),
        "w_br_conv": nrm(ks[9], (DEPTH, D_CONV, D_MODEL), D_CONV ** -0.5),
        "w_br_pool": nrm(ks[10], (DEPTH, D_POOL, D_MODEL), D_POOL ** -0.5),
        "w_o": nrm(ks[11], (DEPTH, D_MODEL, D_MODEL), D_MODEL ** -0.5),
        "norm_ffn": 1.0 + nrm(ks[12], (DEPTH, D_MODEL), 0.1),
        "w_gate": nrm(ks[13], (DEPTH, D_MODEL, D_FF), D_MODEL ** -0.5),
        "w_up": nrm(ks[14], (DEPTH, D_MODEL, D_FF), D_MODEL ** -0.5),
        "w_down": nrm(ks[15], (DEPTH, D_FF, D_MODEL), D_FF ** -0.5),
        "norm_final": 1.0 + nrm(ks[16], (D_MODEL,), 0.1),
    }


def reference(x_prompt, x_sample, state_conv, state_pool, norm_mix, w_in, conv_w, w_pool, pool_scale,
              w_br_conv, w_br_pool, w_o, norm_ffn, w_gate, w_up, w_down, norm_final):
    Bp, Tp, _ = x_prompt.shape
    Ts = x_sample.shape[1]
    counts_p = pool_counts(0, Tp)
    counts_s = pool_counts(PAST_LEN, Ts)
    yp, ys = x_prompt, x_sample
    conv_p, pool_p, conv_s, pool_s = [], [], [], []
    for l in range(DEPTH):
        params = (norm_mix[l], w_in[l], conv_w[l], w_pool[l], pool_scale[l], w_br_conv[l], w_br_pool[l],
                  w_o[l], norm_ffn[l], w_gate[l], w_up[l], w_down[l])
        zc = jnp.zeros((Bp, CONV_WIDTH - 1, D_CONV), x_prompt.dtype)
        zp = jnp.zeros((Bp, POOL_BUF, D_POOL), x_prompt.dtype)
        yp, cp, pp = decoder_layer(yp, zc, zp, counts_p, *params)
        ys, cs_, ps_ = decoder_layer(ys, state_conv[l], state_pool[l], counts_s, *params)
        conv_p.append(cp); pool_p.append(pp); conv_s.append(cs_); pool_s.append(ps_)
    y_prompt = rmsnorm(yp, norm_final)
    y_sample = rmsnorm(ys, norm_final)
    new_conv_prompt = jnp.stack(conv_p, axis=0)
    new_pool_prompt = jnp.stack(pool_p, axis=0)
    new_conv_sample = jnp.stack(conv_s, axis=0)
    new_pool_sample = jnp.stack(pool_s, axis=0)
    return (y_prompt, y_sample, new_conv_prompt, new_pool_prompt, new_conv_sample, new_pool_sample)
```

```python
import numpy as np
from contextlib import ExitStack

import concourse.bass as bass
import concourse.mybir as mybir
from concourse.bass_utils import run_bass_kernel_spmd

F32 = mybir.dt.float32
F32R = mybir.dt.float32r
BF16 = mybir.dt.bfloat16
AF = mybir.ActivationFunctionType
ALU = mybir.AluOpType

D = 2048
NCH = 16
D_POOL = 1024
D_FF = 5632
NFF = 44
D_IN = 11264
EPS = 1e-6
POOL_W = (2, 4, 8, 16)
NPASS = 2
HALO = 16
NPROMPT = 512
NSAMP = 8
NST = 32
WF = 560
WM = 544
TF = 280
TM = 272
SLOT_F = 4096
NSLOT = 6
NSCR = 8

G_MIX, G_FFN, G_FIN, G_CW, G_PS = 0, 16, 32, 48, 96
NGAIN = 104

MM_DT = BF16


class Res:
    __slots__ = ("name", "w", "r")

    def __init__(self, name):
        self.name = name
        self.w = None
        self.r = {}


class DSem:
    def __init__(self, key, h):
        self.key = key
        self.h = h
        self.val = 0


class Prog:
    ENG = ("pe", "act", "dve", "pool", "sp")

    def __init__(self, nc, stack):
        self.nc = nc
        self.stack = stack
        self.ops = {e: [] for e in self.ENG}
        self.semh = {}
        self.cnt = {e: 0 for e in self.ENG}
        self.known = {e: {} for e in self.ENG}
        self.snap = {}
        for e in self.ENG:
            self.semh[e] = stack.enter_context(nc.semaphore("sem_" + e))
        self.out_events = []
        self.n_dsem = 0
        self.sp_sems = [self.new_dsem() for _ in range(8)]
        self.sp_rr = 0

    def new_dsem(self):
        key = "d%d" % self.n_dsem
        self.n_dsem += 1
        h = self.stack.enter_context(self.nc.semaphore("sem_" + key))
        self.semh[key] = h
        return DSem(key, h)

    @staticmethod
    def _add(deps, ev):
        if ev is None:
            return
        k, v = ev
        if deps.get(k, 0) < v:
            deps[k] = v

    def _deps(self, reads, writes):
        deps = {}
        for r in reads:
            self._add(deps, r.w)
        for w in writes:
            self._add(deps, w.w)
            for k, v in w.r.items():
                self._add(deps, (k, v))
        return deps

    def _waits(self, eng, deps):
        kn = self.known[eng]
        for k, v in sorted(deps.items(), key=lambda kv: kv[0] == eng):
            if kn.get(k, 0) >= v:
                continue
            semh = self.semh[k]
            self.ops[eng].append(lambda e, s=semh, v=v: e.wait_ge(s, v))
            kn[k] = v
            sn = self.snap.get((k, v))
            if sn:
                for kk, vv in sn.items():
                    if kn.get(kk, 0) < vv:
                        kn[kk] = vv

    def _update(self, ev, reads, writes):
        for w in writes:
            w.w = ev
            w.r = {}
        for r in reads:
            if r in writes:
                continue
            k, v = ev
            if r.r.get(k, 0) < v:
                r.r[k] = v

    def emit(self, eng, fn, reads=(), writes=()):
        deps = self._deps(reads, writes)
        self._waits(eng, deps)
        self.cnt[eng] += 1
        v = self.cnt[eng]
        sem = self.semh[eng]
        self.ops[eng].append(lambda e, fn=fn, sem=sem: fn(e).then_inc(sem, 1))
        ev = (eng, v)
        self.snap[ev] = dict(self.known[eng])
        self._update(ev, reads, writes)
        return ev

    def dma(self, q, fn, reads=(), writes=(), dsem=None, is_output=False):
        if dsem is None:
            dsem = self.sp_sems[self.sp_rr % len(self.sp_sems)]
            self.sp_rr += 1
        deps = self._deps(reads, writes)
        if dsem.val > 0:
            self._add(deps, (dsem.key, dsem.val))
        self._waits(q, deps)
        dsem.val += 16
        v = dsem.val
        self.ops[q].append(lambda e, fn=fn, s=dsem.h: fn(e).then_inc(s, 16))
        ev = (dsem.key, v)
        self.snap[ev] = dict(self.known[q])
        self._update(ev, reads, writes)
        if is_output:
            self.out_events.append(ev)
        return ev

    def finish(self):
        deps = {}
        for ev in self.out_events:
            self._add(deps, ev)
        self._waits("sp", deps)


class Pool:
    def __init__(self, aps, name):
        self.aps = aps
        self.res = [Res("%s%d" % (name, i)) for i in range(len(aps))]
        self.free_list = list(range(len(aps)))

    def alloc(self):
        assert self.free_list, "scratch pool exhausted"
        return self.free_list.pop(0)

    def free(self, i):
        assert i not in self.free_list
        self.free_list.append(i)


def build_program(debug=None):
    nc = bass.Bass("TRN2", target_bir_lowering=False)

    def din(name, shape):
        return nc.dram_tensor(name, list(shape), F32, kind="ExternalInput").ap()

    def dout(name, shape):
        return nc.dram_tensor(name, list(shape), F32, kind="ExternalOutput").ap()

    xin = din("xin", [NPASS, WF, D])
    sconv = din("sconv", [NPASS, 2 * NSAMP, D])
    spool = din("spool", [NPASS, 15 * NSAMP, D_POOL])
    gains_d = din("gains", [128, NGAIN])
    icnt_d = din("icnt", [NPASS, 128, 64])
    ident_d = din("ident", [128, 128])
    gfin_d = din("gfin_bc", [128, D])
    gmix_d = din("gmix_bc", [128, D])
    w_in = din("w_in", [D, D_IN])
    w_pool = din("w_pool", [4, 256, 256])
    w_br_conv = din("w_br_conv", [D, D])
    w_br_pool = din("w_br_pool", [D_POOL, D])
    w_o = din("w_o", [D, D])
    w_gate = din("w_gate", [D, D_FF])
    w_up = din("w_up", [D, D_FF])
    w_down = din("w_down", [D_FF, D])

    y_p = dout("y_p", [NPASS, NPROMPT, D])
    y_s = dout("y_s", [NPASS, NST, D])
    conv_o = dout("conv_o", [NPASS, 2 + 2 * NSAMP, D])
    pool_op = dout("pool_op", [NPASS, 15, D_POOL])
    pool_os = dout("pool_os", [NPASS, NSAMP, 15, D_POOL])
    dbg_out = {}
    if debug:
        for name, shape in debug.items():
            dbg_out[name] = dout("dbg_" + name, shape)

    w_in_v = w_in.rearrange("(kc p) e -> p kc e", p=128)
    w_brc_v = w_br_conv.rearrange("(kc p) e -> p kc e", p=128)
    w_brp_v = w_br_pool.rearrange("(kc p) e -> p kc e", p=128)
    w_o_v = w_o.rearrange("(kc p) e -> p kc e", p=128)
    w_gate_v = w_gate.rearrange("(kc p) e -> p kc e", p=128)
    w_up_v = w_up.rearrange("(kc p) e -> p kc e", p=128)
    w_down_v = w_down.rearrange("(kc p) e -> p kc e", p=128)
    w_pool_v = w_pool.rearrange("g (kc p) e -> p (g kc) e", p=128)

    with ExitStack() as stack:
        def sb(name, shape, dt=F32):
            return stack.enter_context(nc.sbuf_tensor(name, list(shape), dt))

        RA = sb("RA", [128, NCH, WF], MM_DT)
        RB = sb("RB", [128, NCH, WM], MM_DT)
        RC = sb("RC", [128, NCH, WM], MM_DT)
        RM = sb("RM", [128, 8, WM], MM_DT)
        RP = sb("RP", [128, 4, WM], MM_DT)
        FX = sb("FX", [128, NCH, WF])
        XSB = sb("XSB", [128, 2, D])
        WPOOL = sb("WPOOL", [128, 8, 256], MM_DT)
        RING = sb("RING", [128, NSLOT, SLOT_F], MM_DT)
        SCR = sb("SCR", [128, NSCR, WF])
        UE = sb("UE", [128, NCH, NSAMP, 6])
        VE = sb("VE", [128, 8, NSAMP, 19])
        STU = sb("STU", [128, NCH, 18])
        STV = sb("STV", [128, 8, 47])
        GAINS = sb("GAINS", [128, NGAIN])
        ICNT = sb("ICNT", [128, 64])
        IDENT = sb("IDENT", [128, 128])
        ONES = sb("ONES", [128, 128])
        EPS_T = sb("EPS_T", [128, 1])
        SS = sb("SS", [128, 2, NSAMP, 19])
        T16 = sb("T16", [128, 16])
        SSQ = sb("SSQ", [128, 8])
        SSQ2 = sb("SSQ2", [128, 2, 4])
        GBC = sb("GBC", [128, D])
        IDENTB = sb("IDENTB", [128, 128], BF16)
        PS = [stack.enter_context(nc.psum_tensor("PS%d" % g, [128, 1024], F32)) for g in range(4)]

        P = Prog(nc, stack)
        slot_sems = [P.new_dsem() for _ in range(NSLOT)]
        misc_sem = P.new_dsem()

        rA = [Res("A%d" % j) for j in range(NCH)]
        rB = [Res("B%d" % j) for j in range(NCH)]
        rC = [Res("C%d" % j) for j in range(NCH)]
        rXS = [Res("XS0"), Res("XS1")]
        rM = [Res("M%d" % j) for j in range(8)]
        rP = [Res("P%d" % j) for j in range(4)]
        rF = [Res("F%d" % j) for j in range(NCH)]
        rSLOT = [Res("slot%d" % s) for s in range(NSLOT)]
        rPS = [Res("PS%d" % g) for g in range(4)]
        rUE = Res("UE")
        rVE = Res("VE")
        rSTU = Res("STU")
        rSTV = Res("STV")
        rCONST = Res("CONST")
        rGATE = Res("GATE")
        rWP = Res("WPOOL")
        rICNT = Res("ICNT")
        rSS = [Res("SS0"), Res("SS1")]
        rT16 = Res("T16")
        rSSQ = Res("SSQ")
        rSSQ2 = [Res("SSQ2a"), Res("SSQ2b")]
        rGBC = Res("GBC")
        scr = Pool([SCR[:, i, :] for i in range(NSCR)], "scr")

        A_r = RA[:]
        B_r = RB[:]
        C_r = RC[:]

        def XS(b):
            return XSB[:, b, :]

        def MIX_r(c):
            return RM[:, c, :]

        def split2(ap):
            return ap.rearrange("p (t c) -> p t c", t=2)

        def ps2(g, n):
            return PS[g][:, :].rearrange("p (t c) -> p t c", t=2)[:, :, 0:n]

        def gcol(c):
            return GAINS[:, c:c + 1]

        state = {"pg": 0, "slot": 0, "evac": 0}

        def next_pg():
            g = state["pg"] % 4
            state["pg"] += 1
            return g

        def dve(fn, reads, writes):
            return P.emit("dve", fn, reads, writes)

        def act(fn, reads, writes):
            return P.emit("act", fn, reads, writes)

        def copy_any(out, in_, reads, writes, same_engine=False):
            if not same_engine:
                state["evac"] += 1
            if state["evac"] % 2:
                return act(lambda e: e.activation(out, in_, AF.Copy), reads, writes)
            return dve(lambda e: e.tensor_copy(out, in_), reads, writes)

        def load_slab(dram_ap, kc, ncols):
            s = state["slot"] % NSLOT
            gated = (rGATE,) if 2 <= state["slot"] < NSLOT else ()
            state["slot"] += 1
            view = RING[:, s, 0:kc * ncols].rearrange("p (k c) -> p k c", k=kc)
            P.dma("pool", lambda e: e.dma_start(out=view, in_=dram_ap), reads=gated, writes=(rSLOT[s],),
                  dsem=slot_sems[s])
            return s, view

        def mm_group(klist, n, reads):
            g = next_pg()
            nk = len(klist)

            def fn(e):
                last = None
                for ki, (lhsT, rhs_fn) in enumerate(klist):
                    for t in range(2):
                        last = e.matmul(PS[g][:, 512 * t:512 * t + n], lhsT=lhsT, rhs=rhs_fn(t),
                                        start=(ki == 0), stop=(ki == nk - 1))
                return last
            P.emit("pe", fn, reads, (rPS[g],))
            return g

        def dbg_dump(name, ap_in, reads, idx=None):
            if debug and name in debug:
                o = dbg_out[name]
                if idx is not None:
                    o = o[idx]
                P.dma("sp", lambda e: e.dma_start(out=o, in_=ap_in), reads=reads, writes=(), is_output=True)

        rEPS = Res("EPS")
        dve(lambda e: e.memset(EPS_T[:], EPS), (), (rEPS,))
        dve(lambda e: e.memset(ONES[:], 1.0), (), (rEPS,))
        wp_sem = P.new_dsem()
        c_sems = [P.new_dsem() for _ in range(3)]

        def load_constants():
            P.dma("sp", lambda e: e.dma_start(out=GBC[:], in_=gmix_d[:, :]), writes=(rGBC,), dsem=c_sems[0])
            P.dma("sp", lambda e: e.dma_start(out=IDENT[:], in_=ident_d[:, :]), writes=(rCONST,), dsem=c_sems[1])
            dve(lambda e: e.tensor_copy(IDENTB[:], IDENT[:]), (rCONST,), (rCONST,))
            P.dma("pool", lambda e: e.dma_start(out=WPOOL[:], in_=w_pool_v), writes=(rWP,), dsem=wp_sem)

        def load_gains():
            P.dma("sp", lambda e: e.dma_start(out=GAINS[:], in_=gains_d[:, :]), writes=(rCONST,), dsem=c_sems[1])

        def transposes_in(src_b, rows, ncol_chunks, reads_extra=()):
            out = []
            for c0 in range(0, ncol_chunks, 8):
                ncc = min(8, ncol_chunks - c0)
                g = next_pg()

                def fn(e, c0=c0, ncc=ncc, g=g):
                    last = None
                    for c in range(ncc):
                        last = e.transpose(PS[g][:, c * 128:c * 128 + rows],
                                           XS(src_b)[0:rows, (c0 + c) * 128:(c0 + c + 1) * 128],
                                           IDENT[0:rows, 0:rows])
                    return last
                P.emit("pe", fn, (rXS[src_b], rCONST) + tuple(reads_extra), (rPS[g],))
                out.append((g, c0, ncc))
            return out

        def XNb(b):
            return RM[:, 4 * b:4 * b + 4, :].rearrange("p a c -> p (a c)")[:, 0:D]

        def tile_rows(ti):
            rows = 48 if ti == 0 else 128
            r0 = 0 if ti == 0 else 48 + 128 * (ti - 1)
            return rows, r0

        def tile_stage(ti, nstage):
            sidx = ti % nstage
            if sidx < 2:
                return XS(sidx), (rXS[sidx],)
            c0 = 4 * (sidx - 2)
            return FX[:, c0:c0 + 4, :].rearrange("p a c -> p (a c)")[:, 0:D], tuple(rF[c0:c0 + 4])

        def norm_tile_dma(pi, ti, nstage=2):
            rows, r0 = tile_rows(ti)
            xs_ap, xs_res = tile_stage(ti, nstage)
            gate = (rGATE,) if (nstage == 4 and ti == 4) else ()
            P.dma("sp", lambda e: e.dma_start(out=xs_ap[0:rows, :], in_=xin[pi, r0:r0 + rows, :]),
                  writes=xs_res + gate)

        def norm_tile_A(pi, ti, nstage=2):
            norm_tile_dma(pi, ti, nstage)
            norm_tile_cmp(pi, ti, nstage)

        def norm_tile_cmp(pi, ti, nstage=2):
            b = ti % 2
            rows, r0 = tile_rows(ti)
            rxn = tuple(rM[4 * b:4 * b + 4])
            xs_ap, xs_res = tile_stage(ti, nstage)
            act(lambda e: e.activation(XNb(b)[0:rows, :], xs_ap[0:rows, :], AF.Square,
                                       accum_out=SSQ2[0:rows, b, 0:1]),
                xs_res, rxn + (rSSQ2[b],))
            act(lambda e: e.activation(SSQ2[0:rows, b, 1:2], SSQ2[0:rows, b, 0:1], AF.Sqrt,
                                       bias=EPS_T[0:rows, 0:1], scale=1.0 / D),
                (rSSQ2[b], rEPS), (rSSQ2[b],))
            dve(lambda e: e.reciprocal(SSQ2[0:rows, b, 2:3], SSQ2[0:rows, b, 1:2]), (rSSQ2[b],), (rSSQ2[b],))
            dve(lambda e: e.scalar_tensor_tensor(XNb(b)[0:rows, :], xs_ap[0:rows, :], SSQ2[0:rows, b, 2:3],
                                                 GBC[0:rows, :], ALU.mult, ALU.mult),
                xs_res + (rSSQ2[b], rGBC), rxn)

        def norm_tile_B(pi, ti):
            b = ti % 2
            rows, r0 = tile_rows(ti)
            rxn = tuple(rM[4 * b:4 * b + 4])
            g = next_pg()
            psb = PS[g][:, :].bitcast(BF16)

            def fn(e):
                last = None
                for c in range(NCH):
                    last = e.transpose(psb[:, c * 128:c * 128 + rows], XNb(b)[0:rows, c * 128:(c + 1) * 128],
                                       IDENTB[0:rows, 0:rows])
                return last
            P.emit("pe", fn, rxn + (rCONST,), (rPS[g],))
            pv = psb.rearrange("p (c k) -> p c k", c=NCH)
            if ti == 0:
                copy_any(A_r[:, :, 0:HALO], pv[:, :, 0:HALO], (rPS[g],), tuple(rA))
                copy_any(A_r[:, :, HALO + NPROMPT:WF], pv[:, :, HALO:48], (rPS[g],), tuple(rA), same_engine=True)
            else:
                o = HALO + 128 * (ti - 1)
                copy_any(A_r[:, :, o:o + 128], pv[:, :, 0:128], (rPS[g],), tuple(rA))

        def s0_steps(pi, last, nstage=2):
            def st_dma():
                P.dma("sp", lambda e: e.dma_start(out=XS(0)[0:16, :], in_=sconv[pi, :, :]), writes=(rXS[0],))
                P.dma("sp", lambda e: e.dma_start(out=XS(1)[0:120, 0:D_POOL], in_=spool[pi, :, :]),
                      writes=(rXS[1],))
                P.dma("sp", lambda e: e.dma_start(
                    out=pool_os[pi, :, 0:11, :],
                    in_=spool[pi, :, :].rearrange("(s t) c -> s t c", t=15)[:, 4:15, :]), is_output=True)

            def st_tr():
                for (g, c0, ncc) in transposes_in(0, 16, NCH):
                    pv = PS[g][:, :].rearrange("p (c k) -> p c k", c=8)[:, 0:ncc, 0:16]
                    copy_any(UE[:, c0:c0 + ncc, :, 0:2], pv.rearrange("p c (s t) -> p c s t", t=2),
                             (rPS[g],), (rUE,))
                for (g, c0, ncc) in transposes_in(1, 120, 8):
                    pv = PS[g][:, :].rearrange("p (c k) -> p c k", c=8)[:, 0:ncc, 0:120]
                    copy_any(VE[:, c0:c0 + ncc, :, 0:15], pv.rearrange("p c (s t) -> p c s t", t=15),
                             (rPS[g],), (rVE,))

            def fin():
                norm_tile_B(pi, 4)
                if last:
                    P.dma("sp", lambda e: e.dma_start(out=GBC[:], in_=gfin_d[:, :]), writes=(rGBC,),
                          dsem=misc_sem)
            return [
                st_dma,
                st_tr,
                lambda: (norm_tile_A(pi, 0, nstage), norm_tile_A(pi, 1, nstage)),
                lambda: (norm_tile_B(pi, 0), norm_tile_A(pi, 2, nstage)),
                lambda: (norm_tile_B(pi, 1), norm_tile_A(pi, 3, nstage)),
                lambda: (norm_tile_B(pi, 2), norm_tile_A(pi, 4, nstage)),
                lambda: norm_tile_B(pi, 3),
                fin,
            ]

        def raw_dma(pi, ti):
            b = (ti + 1) % 2
            rows, r0 = tile_rows(ti)
            P.dma("sp", lambda e: e.dma_start(out=XS(b)[0:rows, :], in_=xin[pi, r0:r0 + rows, :]),
                  writes=(rXS[b],))

        def load_raw_tile(pi, ti):
            b = (ti + 1) % 2
            rows, r0 = tile_rows(ti)
            for (g, c0, ncc) in transposes_in(b, rows, NCH):
                pv = PS[g][:, :].rearrange("p (c k) -> p c k", c=8)
                wr = tuple(rF[c0:c0 + ncc])
                if ti == 0:
                    copy_any(FX[:, c0:c0 + ncc, HALO + NPROMPT:WF], pv[:, 0:ncc, HALO:48], (rPS[g],), wr)
                else:
                    o = HALO + 128 * (ti - 1)
                    copy_any(FX[:, c0:c0 + ncc, o:o + 128], pv[:, 0:ncc, 0:128], (rPS[g],), wr)
            if ti + 2 < 5:
                raw_dma(pi, ti + 2)

        class Norm:
            def __init__(self, src_fn, src_res, width, gain0, dst_fn, dst_res):
                self.src_fn, self.src_res, self.width = src_fn, src_res, width
                self.gain0, self.dst_fn, self.dst_res = gain0, dst_fn, dst_res
                self.acc = None

            def add(self, j):
                width = self.width
                if self.acc is None:
                    self.acc = scr.alloc()
                    acc_ap = scr.aps[self.acc][:, 0:width]
                    act(lambda e: e.activation(acc_ap, self.src_fn(j), AF.Square),
                        (self.src_res[j],), (scr.res[self.acc],))
                    return
                acc_ap = scr.aps[self.acc][:, 0:width]
                sq = scr.alloc()
                sq_ap = scr.aps[sq][:, 0:width]
                act(lambda e: e.activation(sq_ap, self.src_fn(j), AF.Square), (self.src_res[j],), (scr.res[sq],))
                dve(lambda e: e.tensor_tensor(acc_ap, acc_ap, sq_ap, ALU.add),
                    (scr.res[sq], scr.res[self.acc]), (scr.res[self.acc],))
                scr.free(sq)

            def finish(self):
                rb = self.finish_stats()
                rb_ap = scr.aps[rb][:, 0:self.width]
                for j in range(NCH):
                    rd = (self.src_res[j], scr.res[rb], rCONST)
                    wr = (self.dst_res[j],)
                    dve(lambda e, j=j: e.scalar_tensor_tensor(self.dst_fn(j), self.src_fn(j),
                                                              gcol(self.gain0 + j), rb_ap, ALU.mult, ALU.mult),
                        rd, wr)
                scr.free(rb)

            def finish_stats(self):
                width = self.width
                tw = width // 2
                acc = self.acc
                acc_ap = scr.aps[acc][:, 0:width]
                g = next_pg()

                def fn(e):
                    last = None
                    for t in range(2):
                        last = e.matmul(PS[g][:, 512 * t:512 * t + tw], lhsT=ONES[:, :],
                                        rhs=acc_ap[:, tw * t:tw * (t + 1)], start=True, stop=True)
                    return last
                P.emit("pe", fn, (scr.res[acc], rEPS), (rPS[g],))
                rt = scr.alloc()
                rt_ap = scr.aps[rt][:, 0:width]
                act(lambda e: e.activation(split2(rt_ap), ps2(g, tw), AF.Sqrt, bias=EPS_T[:, 0:1], scale=1.0 / D),
                    (rPS[g], rEPS), (scr.res[rt],))
                scr.free(acc)
                rb = scr.alloc()
                rb_ap = scr.aps[rb][:, 0:width]
                dve(lambda e: e.reciprocal(rb_ap, rt_ap), (scr.res[rt],), (scr.res[rb],))
                scr.free(rt)
                return rb

        def emit_s8_tile(pi, tt):
            ncols = 128 if tt < 4 else NST
            c0 = 128 * tt
            b = tt % 2
            gs = []
            for half in range(2):
                g = next_pg()

                def fn(e, half=half, g=g, ncols=ncols, c0=c0):
                    last = None
                    for c in range(8):
                        last = e.transpose(PS[g][0:ncols, c * 128:(c + 1) * 128],
                                           FX[:, 8 * half + c, HALO + c0:HALO + c0 + ncols], IDENT[:, :])
                    return last
                P.emit("pe", fn, tuple(rF[8 * half:8 * half + 8]) + (rCONST,), (rPS[g],))
                act(lambda e, g=g, half=half, ncols=ncols, b=b: e.activation(
                    XS(b)[0:ncols, 1024 * half:1024 * (half + 1)], PS[g][0:ncols, :], AF.Square,
                    accum_out=SSQ[0:ncols, half:half + 1]),
                    (rPS[g],), (rXS[b], rSSQ))
                gs.append(g)
            dve(lambda e, ncols=ncols: e.tensor_tensor(SSQ[0:ncols, 2:3], SSQ[0:ncols, 0:1], SSQ[0:ncols, 1:2],
                                                       ALU.add), (rSSQ,), (rSSQ,))
            act(lambda e, ncols=ncols: e.activation(SSQ[0:ncols, 3:4], SSQ[0:ncols, 2:3], AF.Sqrt,
                                                    bias=EPS_T[0:ncols, 0:1], scale=1.0 / D),
                (rSSQ, rEPS), (rSSQ,))
            dve(lambda e, ncols=ncols: e.reciprocal(SSQ[0:ncols, 4:5], SSQ[0:ncols, 3:4]), (rSSQ,), (rSSQ,))
            for half in range(2):
                g = gs[half]
                dve(lambda e, g=g, half=half, ncols=ncols, b=b: e.scalar_tensor_tensor(
                    XS(b)[0:ncols, 1024 * half:1024 * (half + 1)], PS[g][0:ncols, :], SSQ[0:ncols, 4:5],
                    GBC[0:ncols, 1024 * half:1024 * (half + 1)], ALU.mult, ALU.mult),
                    (rPS[g], rSSQ, rGBC), (rXS[b],))
            if tt < 4:
                P.dma("sp", lambda e, pi=pi, b=b, c0=c0: e.dma_start(out=y_p[pi, c0:c0 + 128, :],
                                                                     in_=XS(b)[0:128, :]),
                      reads=(rXS[b],), is_output=True)
            else:
                P.dma("sp", lambda e, pi=pi, b=b: e.dma_start(out=y_s[pi, :, :], in_=XS(b)[0:NST, :]),
                      reads=(rXS[b],), is_output=True)


        s8_pending = [None]
        for pi in range(NPASS):
            late_steps = []
            if pi == 0:
                st = s0_steps(0, NPASS == 1, nstage=4)
                norm_tile_dma(0, 0, 4)
                norm_tile_dma(0, 1, 4)
                load_constants()
                norm_tile_dma(0, 2, 4)
                norm_tile_dma(0, 3, 4)
                load_gains()
                norm_tile_cmp(0, 0, 4)
                norm_tile_cmp(0, 1, 4)
                norm_tile_B(0, 0)
                norm_tile_dma(0, 4, 4)
                norm_tile_cmp(0, 2, 4)
                norm_tile_B(0, 1)
                norm_tile_cmp(0, 3, 4)
                norm_tile_B(0, 2)
                norm_tile_cmp(0, 4, 4)
                norm_tile_B(0, 3)
                st[7]()
                st[0]()
                late_steps = [st[1]]
            P.dma("sp", lambda e, pi=pi: e.dma_start(out=ICNT[:], in_=icnt_d[pi, :, :]), writes=(rICNT,),
                  dsem=misc_sem)

            def A_rhs_full(k):
                return lambda t: A_r[:, k, TF * t:TF * (t + 1)]

            def A_rhs_main(k):
                return lambda t: A_r[:, k, HALO + TM * t:HALO + TM * (t + 1)]

            for jp in range(NCH // 2):
                js = (2 * jp, 2 * jp + 1)
                if s8_pending[0] is not None and jp < 5:
                    emit_s8_tile(s8_pending[0], jp)
                    if jp == 4:
                        s8_pending[0] = None
                if jp == NCH // 2 - 1:
                    raw_dma(pi, 0)
                sH, vH = load_slab(w_in_v[:, :, 256 * jp:256 * jp + 256], NCH, 256)
                hs = []
                for jj, j in enumerate(js):
                    g = mm_group([(vH[:, k, 128 * jj:128 * jj + 128], A_rhs_full(k)) for k in range(NCH)], TF,
                                 tuple(rA) + (rSLOT[sH],))
                    t = scr.alloc()
                    act(lambda e, t=t, g=g: e.activation(split2(scr.aps[t][:, 0:WF]), ps2(g, TF), AF.Copy),
                        (rPS[g],), (scr.res[t],))
                    hs.append(t)
                while late_steps:
                    late_steps.pop(0)()
                sC, vC = load_slab(w_in_v[:, :, 4096 + 256 * jp:4096 + 256 * jp + 256], NCH, 256)
                cvs = []
                for jj, j in enumerate(js):
                    g = mm_group([(vC[:, k, 128 * jj:128 * jj + 128], A_rhs_full(k)) for k in range(NCH)], TF,
                                 tuple(rA) + (rSLOT[sC],))
                    u = scr.alloc()
                    u_ap = scr.aps[u]
                    h_ap = scr.aps[hs[jj]]
                    dve(lambda e, u_ap=u_ap, h_ap=h_ap, g=g: e.tensor_tensor(
                        split2(u_ap[:, 0:WF]), ps2(g, TF), split2(h_ap[:, 0:WF]), ALU.mult),
                        (rPS[g], scr.res[hs[jj]]), (scr.res[u],))
                    scr.free(hs[jj])
                    act(lambda e, u_ap=u_ap, j=j: e.activation(
                        UE[:, j, :, 2:6], u_ap[:, HALO + NPROMPT:WF].rearrange("p (s t) -> p s t", t=4), AF.Copy),
                        (scr.res[u],), (rUE,))
                    act(lambda e, u_ap=u_ap, j=j: e.activation(
                        STU[:, j, 0:2], u_ap[:, HALO + NPROMPT - 2:HALO + NPROMPT], AF.Copy),
                        (scr.res[u],), (rSTU,))
                    cv = scr.alloc()
                    cv_ap = scr.aps[cv]
                    cw = [gcol(G_CW + 16 * k + j) for k in range(3)]
                    p0, p1 = HALO, HALO + NPROMPT
                    dve(lambda e, u_ap=u_ap, cv_ap=cv_ap, cw=cw: e.tensor_scalar(
                        cv_ap[:, p0:p1], u_ap[:, p0 - 2:p1 - 2], cw[0], None, ALU.mult),
                        (scr.res[u], rCONST), (scr.res[cv],))
                    dve(lambda e, u_ap=u_ap, cv_ap=cv_ap, cw=cw: e.scalar_tensor_tensor(
                        cv_ap[:, p0:p1], u_ap[:, p0 - 1:p1 - 1], cw[1], cv_ap[:, p0:p1], ALU.mult, ALU.add),
                        (scr.res[u], rCONST, scr.res[cv]), (scr.res[cv],))
                    dve(lambda e, u_ap=u_ap, cv_ap=cv_ap, cw=cw: e.scalar_tensor_tensor(
                        cv_ap[:, p0:p1], u_ap[:, p0:p1], cw[2], cv_ap[:, p0:p1], ALU.mult, ALU.add),
                        (scr.res[u], rCONST, scr.res[cv]), (scr.res[cv],))
                    scr.free(u)
                    cvs_ap = cv_ap[:, p1:WF].rearrange("p (s t) -> p s t", t=4)
                    dve(lambda e, cvs_ap=cvs_ap, cw=cw, j=j: e.tensor_scalar(
                        cvs_ap, UE[:, j, :, 0:4], cw[0], None, ALU.mult),
                        (rUE, rCONST, scr.res[cv]), (scr.res[cv],))
                    dve(lambda e, cvs_ap=cvs_ap, cw=cw, j=j: e.scalar_tensor_tensor(
                        cvs_ap, UE[:, j, :, 1:5], cw[1], cvs_ap, ALU.mult, ALU.add),
                        (rUE, rCONST, scr.res[cv]), (scr.res[cv],))
                    dve(lambda e, cvs_ap=cvs_ap, cw=cw, j=j: e.scalar_tensor_tensor(
                        cvs_ap, UE[:, j, :, 2:6], cw[2], cvs_ap, ALU.mult, ALU.add),
                        (rUE, rCONST, scr.res[cv]), (scr.res[cv],))
                    cvs.append(cv)
                sB, vB = load_slab(w_in_v[:, :, 2048 + 256 * jp:2048 + 256 * jp + 256], NCH, 256)
                for jj, j in enumerate(js):
                    g = mm_group([(vB[:, k, 128 * jj:128 * jj + 128], A_rhs_main(k)) for k in range(NCH)], TM,
                                 tuple(rA) + (rSLOT[sB],))
                    cv_ap = scr.aps[cvs[jj]]
                    dve(lambda e, cv_ap=cv_ap, g=g, j=j: e.tensor_tensor(
                        split2(B_r[:, j, :]), ps2(g, TM), split2(cv_ap[:, HALO:WF]), ALU.mult),
                        (rPS[g], scr.res[cvs[jj]]), (rB[j],))
                    scr.free(cvs[jj])
            act(lambda e: e.activation(STU[:, :, 2:18].rearrange("p c (s t) -> p c s t", t=2), UE[:, :, :, 4:6],
                                       AF.Copy), (rUE,), (rSTU,))
            for half in range(2):
                g = next_pg()

                def fn(e, half=half, g=g):
                    last = None
                    for c in range(8):
                        last = e.transpose(PS[g][0:18, c * 128:(c + 1) * 128], STU[:, 8 * half + c, :],
                                           IDENT[:, :])
                    return last
                P.emit("pe", fn, (rSTU, rCONST), (rPS[g],))
                copy_any(XS(0)[0:18, 1024 * half:1024 * (half + 1)], PS[g][0:18, :], (rPS[g],), (rXS[0],))
            P.dma("sp", lambda e, pi=pi: e.dma_start(out=conv_o[pi, :, :], in_=XS(0)[0:18, :]),
                  reads=(rXS[0],), is_output=True)
            raw_dma(pi, 1)

            vWP = WPOOL[:]

            def emit_s3(grp):
                pls = [(grp % 2) * 2, (grp % 2) * 2 + 1]
                for ee in range(2):
                    c = 2 * grp + ee
                    klist = []
                    for kc in range(2):
                        pl_r = RP[:, pls[kc], :]
                        klist.append((vWP[:, 2 * grp + kc, 128 * ee:128 * ee + 128],
                                      (lambda t, pl_r=pl_r: pl_r[:, TM * t:TM * (t + 1)])))
                    g = mm_group(klist, TM, (rP[pls[0]], rP[pls[1]], rWP))
                    act(lambda e, g=g, c=c: e.activation(split2(MIX_r(c)), ps2(g, TM), AF.Copy,
                                                         scale=gcol(G_PS + c)),
                        (rPS[g], rCONST), (rM[c],))
            for jp in range(4):
                grp = jp
                w = POOL_W[grp]
                nstep = {2: 1, 4: 2, 8: 3, 16: 4}[w]
                sV, vV = load_slab(w_in_v[:, :, 6144 + 256 * jp:6144 + 256 * jp + 256], NCH, 256)
                pls = []
                for jj in range(2):
                    j = 2 * jp + jj
                    g = mm_group([(vV[:, k, 128 * jj:128 * jj + 128], A_rhs_full(k)) for k in range(NCH)], TF,
                                 tuple(rA) + (rSLOT[sV],))
                    vv = scr.alloc()
                    vv_ap = scr.aps[vv]
                    act(lambda e, vv_ap=vv_ap, g=g: e.activation(split2(vv_ap[:, 0:WF]), ps2(g, TF), AF.Copy),
                        (rPS[g],), (scr.res[vv],))
                    act(lambda e, vv_ap=vv_ap, j=j: e.activation(
                        VE[:, j, :, 15:19], vv_ap[:, HALO + NPROMPT:WF].rearrange("p (s t) -> p s t", t=4), AF.Copy),
                        (scr.res[vv],), (rVE,))
                    act(lambda e, vv_ap=vv_ap, j=j: e.activation(
                        STV[:, j, 0:15], vv_ap[:, HALO + NPROMPT - 15:HALO + NPROMPT], AF.Copy),
                        (scr.res[vv],), (rSTV,))
                    PE_ = HALO + NPROMPT
                    cur = vv
                    tmps = []
                    for st in range(nstep):
                        sh = 1 << st
                        lo = (1 << (st + 1)) - 1
                        nx = scr.alloc()
                        dve(lambda e, a=scr.aps[cur], o=scr.aps[nx], sh=sh, lo=lo: e.tensor_tensor(
                            o[:, lo:PE_], a[:, lo:PE_], a[:, lo - sh:PE_ - sh], ALU.add),
                            (scr.res[cur],), (scr.res[nx],))
                        if cur != vv:
                            scr.free(cur)
                        cur = nx
                    pl = (jp % 2) * 2 + jj
                    pl_r = RP[:, pl, :]
                    dve(lambda e, s_ap=scr.aps[cur], vv_ap=vv_ap, pl_r=pl_r, w=w: e.scalar_tensor_tensor(
                        pl_r[:, 0:NPROMPT], s_ap[:, HALO:PE_], 1.0 / w, vv_ap[:, HALO:PE_], ALU.mult, ALU.subtract),
                        (scr.res[cur], scr.res[vv]), (rP[pl],))
                    dve(lambda e, s_ap=scr.aps[cur], grp=grp: e.tensor_tensor(
                        T16[:, :], s_ap[:, HALO:HALO + 16], ICNT[:, 16 * grp:16 * grp + 16], ALU.mult),
                        (scr.res[cur], rICNT), (rT16,))
                    dve(lambda e, vv_ap=vv_ap, pl_r=pl_r: e.tensor_tensor(
                        pl_r[:, 0:16], T16[:, :], vv_ap[:, HALO:HALO + 16], ALU.subtract),
                        (rT16, scr.res[vv], rP[pl]), (rP[pl],))
                    scr.free(cur)
                    curs = None
                    for st in range(nstep):
                        sh = 1 << st
                        lo = (1 << (st + 1)) - 1
                        src_ap = VE[:, j, :, :] if curs is None else SS[:, curs, :, :]
                        src_res = rVE if curs is None else rSS[curs]
                        nxs = 0 if curs is None else 1 - curs
                        dve(lambda e, a=src_ap, o=SS[:, nxs, :, :], sh=sh, lo=lo: e.tensor_tensor(
                            o[:, :, lo:19], a[:, :, lo:19], a[:, :, lo - sh:19 - sh], ALU.add),
                            (src_res,), (rSS[nxs],))
                        curs = nxs
                    dve(lambda e, curs=curs, pl_r=pl_r, w=w, j=j: e.scalar_tensor_tensor(
                        pl_r[:, NPROMPT:WM].rearrange("p (s t) -> p s t", t=4), SS[:, curs, :, 15:19], 1.0 / w,
                        VE[:, j, :, 15:19], ALU.mult, ALU.subtract),
                        (rSS[curs], rVE, rP[pl]), (rP[pl],))
                    scr.free(vv)
                    pls.append(pl)
                if jp >= 1:
                    emit_s3(jp - 1)
                load_raw_tile(pi, jp)
            act(lambda e: e.activation(STV[:, :, 15:47].rearrange("p c (s t) -> p c s t", t=4), VE[:, :, :, 15:19],
                                       AF.Copy), (rVE,), (rSTV,))

            def B_rhs(k):
                return lambda t: B_r[:, k, TM * t:TM * (t + 1)]

            def C_rhs(k):
                return lambda t: C_r[:, k, TM * t:TM * (t + 1)]

            def MIX_rhs(k):
                return lambda t: MIX_r(k)[:, TM * t:TM * (t + 1)]

            for jp in range(NCH // 2):
                js = (2 * jp, 2 * jp + 1)
                sG, vG = load_slab(w_in_v[:, :, 7168 + 256 * jp:7168 + 256 * jp + 256], NCH, 256)
                sgc = []
                for jj, j in enumerate(js):
                    g = mm_group([(vG[:, k, 128 * jj:128 * jj + 128], A_rhs_main(k)) for k in range(NCH)], TM,
                                 tuple(rA) + (rSLOT[sG],))
                    t = scr.alloc()
                    act(lambda e, t=t, g=g: e.activation(split2(scr.aps[t][:, 0:WM]), ps2(g, TM), AF.Sigmoid),
                        (rPS[g],), (scr.res[t],))
                    sgc.append(t)
                sY, vY = load_slab(w_brc_v[:, :, 256 * jp:256 * jp + 256], NCH, 256)
                m1 = []
                for jj, j in enumerate(js):
                    g = mm_group([(vY[:, k, 128 * jj:128 * jj + 128], B_rhs(k)) for k in range(NCH)], TM,
                                 tuple(rB) + (rSLOT[sY],))
                    t = scr.alloc()
                    dve(lambda e, t=t, g=g, s=sgc[jj]: e.tensor_tensor(
                        split2(scr.aps[t][:, 0:WM]), ps2(g, TM), split2(scr.aps[s][:, 0:WM]), ALU.mult),
                        (rPS[g], scr.res[sgc[jj]]), (scr.res[t],))
                    scr.free(sgc[jj])
                    m1.append(t)
                sG2, vG2 = load_slab(w_in_v[:, :, 9216 + 256 * jp:9216 + 256 * jp + 256], NCH, 256)
                sgp = []
                for jj, j in enumerate(js):
                    g = mm_group([(vG2[:, k, 128 * jj:128 * jj + 128], A_rhs_main(k)) for k in range(NCH)], TM,
                                 tuple(rA) + (rSLOT[sG2],))
                    t = scr.alloc()
                    act(lambda e, t=t, g=g: e.activation(split2(scr.aps[t][:, 0:WM]), ps2(g, TM), AF.Sigmoid),
                        (rPS[g],), (scr.res[t],))
                    sgp.append(t)
                if jp == 0:
                    emit_s3(3)
                    load_raw_tile(pi, 4)
                sY2, vY2 = load_slab(w_brp_v[:, :, 256 * jp:256 * jp + 256], 8, 256)
                for jj, j in enumerate(js):
                    g = mm_group([(vY2[:, k, 128 * jj:128 * jj + 128], MIX_rhs(k)) for k in range(8)], TM,
                                 tuple(rM) + (rSLOT[sY2],))
                    t = scr.alloc()
                    dve(lambda e, t=t, g=g, s=sgp[jj]: e.tensor_tensor(
                        split2(scr.aps[t][:, 0:WM]), ps2(g, TM), split2(scr.aps[s][:, 0:WM]), ALU.mult),
                        (rPS[g], scr.res[sgp[jj]]), (scr.res[t],))
                    scr.free(sgp[jj])
                    dve(lambda e, t=t, m=m1[jj], j=j: e.tensor_tensor(
                        C_r[:, j, :], scr.aps[t][:, 0:WM], scr.aps[m][:, 0:WM], ALU.add),
                        (scr.res[t], scr.res[m1[jj]]), (rC[j],))
                    scr.free(t)
                    scr.free(m1[jj])

            g = next_pg()

            def fn(e, g=g):
                last = None
                for c in range(8):
                    last = e.transpose(PS[g][0:47, c * 128:(c + 1) * 128], STV[:, c, :], IDENT[:, :])
                return last
            P.emit("pe", fn, (rSTV, rCONST), (rPS[g],))
            copy_any(XS(1)[0:47, 0:D_POOL], PS[g][0:47, :], (rPS[g],), (rXS[1],))
            P.dma("sp", lambda e, pi=pi: e.dma_start(out=pool_op[pi, :, :], in_=XS(1)[0:15, 0:D_POOL]),
                  reads=(rXS[1],), is_output=True)
            for s in range(NSAMP):
                P.dma("sp", lambda e, pi=pi, s=s: e.dma_start(out=pool_os[pi, s, 11:15, :],
                                                              in_=XS(1)[15 + 4 * s:19 + 4 * s, 0:D_POOL]),
                      reads=(rXS[1],), is_output=True)

            nrm = Norm(lambda j: FX[:, j, HALO:WF], rF, WM, G_FFN, lambda j: B_r[:, j, :], rB)
            for jp in range(NCH // 2):
                sO, vO = load_slab(w_o_v[:, :, 256 * jp:256 * jp + 256], NCH, 256)
                for jj in range(2):
                    j = 2 * jp + jj
                    g = mm_group([(vO[:, k, 128 * jj:128 * jj + 128], C_rhs(k)) for k in range(NCH)], TM,
                                 tuple(rC) + (rSLOT[sO],))
                    dve(lambda e, g=g, j=j: e.tensor_tensor(
                        split2(FX[:, j, HALO:WF]), ps2(g, TM), split2(FX[:, j, HALO:WF]), ALU.add),
                        (rPS[g], rF[j]), (rF[j],))
                    act(lambda e, j=j: e.activation(B_r[:, j, :], FX[:, j, HALO:WF], AF.Copy,
                                                    scale=gcol(G_FFN + j)),
                        (rF[j], rCONST), (rB[j],))
                    nrm.add(j)
            if debug and "x1" in debug:
                for j in range(NCH):
                    dbg_dump("x1", FX[:, j, HALO:WF], (rF[j],), idx=(pi, slice(None), j))

            rb6 = nrm.finish_stats()
            rb6_ap = scr.aps[rb6][:, 0:WM]

            def HN_rhs(k):
                return lambda t: B_r[:, k, TM * t:TM * (t + 1)]

            nblk = (NFF + 7) // 8
            def emit_down(fb):
                nf = min(8, NFF - 8 * fb)
                base = (fb % 2) * 8
                for jq in range(4):
                    sD, vD = load_slab(w_down_v[:, 8 * fb:8 * fb + nf, 512 * jq:512 * jq + 512], nf, 512)
                    for jj in range(4):
                        j = 4 * jq + jj
                        g = mm_group([(vD[:, k, 128 * jj:128 * jj + 128], C_rhs(base + k)) for k in range(nf)], TM,
                                     tuple(rC[base:base + nf]) + (rSLOT[sD],))
                        dve(lambda e, g=g, j=j: e.tensor_tensor(
                            split2(FX[:, j, HALO:WF]), ps2(g, TM), split2(FX[:, j, HALO:WF]), ALU.add),
                            (rPS[g], rF[j]), (rF[j],))

            nxt_steps = s0_steps(pi + 1, pi + 2 == NPASS) if pi + 1 < NPASS else []
            pair_no = 0
            for fb in range(nblk):
                nf = min(8, NFF - 8 * fb)
                base = (fb % 2) * 8
                for fp in range(nf // 2):
                    if pair_no % 2 == 1 and nxt_steps:
                        nxt_steps.pop(0)()
                    pair_no += 1
                    col = 1024 * fb + 256 * fp
                    sG, vG = load_slab(w_gate_v[:, :, col:col + 256], NCH, 256)
                    sl = []
                    for ff in range(2):
                        g = mm_group([(vG[:, k, 128 * ff:128 * ff + 128], HN_rhs(k)) for k in range(NCH)], TM,
                                     tuple(rB) + (rSLOT[sG],))
                        t = scr.alloc()
                        t_ap = scr.aps[t][:, 0:WM]
                        dve(lambda e, t_ap=t_ap, g=g, rb6_ap=rb6_ap: e.tensor_tensor(
                            split2(t_ap), ps2(g, TM), split2(rb6_ap), ALU.mult),
                            (rPS[g], scr.res[rb6]), (scr.res[t],))
                        act(lambda e, t_ap=t_ap: e.activation(t_ap, t_ap, AF.Silu), (scr.res[t],), (scr.res[t],))
                        dve(lambda e, t_ap=t_ap, rb6_ap=rb6_ap: e.tensor_tensor(t_ap, t_ap, rb6_ap, ALU.mult),
                            (scr.res[t], scr.res[rb6]), (scr.res[t],))
                        sl.append(t)
                    sU, vU = load_slab(w_up_v[:, :, col:col + 256], NCH, 256)
                    for ff in range(2):
                        slot = base + 2 * fp + ff
                        g = mm_group([(vU[:, k, 128 * ff:128 * ff + 128], HN_rhs(k)) for k in range(NCH)], TM,
                                     tuple(rB) + (rSLOT[sU],))
                        dve(lambda e, g=g, t=sl[ff], slot=slot: e.tensor_tensor(
                            split2(C_r[:, slot, :]), ps2(g, TM), split2(scr.aps[t][:, 0:WM]), ALU.mult),
                            (rPS[g], scr.res[sl[ff]]), (rC[slot],))
                        scr.free(sl[ff])
                if fb >= 1:
                    emit_down(fb - 1)
            emit_down(nblk - 1)
            while nxt_steps:
                nxt_steps.pop(0)()

            if debug and "x2" in debug:
                for j in range(NCH):
                    dbg_dump("x2", FX[:, j, HALO:WF], (rF[j],), idx=(pi, slice(None), j))
            scr.free(rb6)
            if pi == NPASS - 1:
                for tt in range(5):
                    emit_s8_tile(pi, tt)
            else:
                s8_pending[0] = pi

        P.finish()

        with nc.Block() as block:
            @block.tensor
            def _(e):
                for op in P.ops["pe"]:
                    op(e)

            @block.scalar
            def _(e):
                for op in P.ops["act"]:
                    op(e)

            @block.vector
            def _(e):
                for op in P.ops["dve"]:
                    op(e)

            @block.gpsimd
            def _(e):
                for op in P.ops["pool"]:
                    op(e)

            @block.sync
            def _(e):
                for op in P.ops["sp"]:
                    op(e)
    return nc


def _fm(v):
    return np.ascontiguousarray(np.asarray(v, np.float32).reshape(-1, 128).T)


def make_in_maps(x_prompt, x_sample, state_conv, state_pool, norm_mix, w_in, conv_w, w_pool, pool_scale,
                 w_br_conv, w_br_pool, w_o, norm_ffn, w_gate, w_up, w_down, norm_final):
    f = lambda a: np.ascontiguousarray(np.asarray(a, np.float32))
    x_prompt, x_sample = f(x_prompt), f(x_sample)
    state_conv, state_pool = f(state_conv), f(state_pool)
    gains = np.concatenate([_fm(norm_mix[0]), _fm(norm_ffn[0]), _fm(norm_final)] +
                           [_fm(conv_w[0][k]) for k in range(3)] + [_fm(pool_scale[0])], axis=1)
    gains = np.ascontiguousarray(gains, np.float32)
    assert gains.shape == (128, NGAIN)
    ident = np.eye(128, dtype=np.float32)
    shared = {
        "gains": gains, "ident": ident,
        "gfin_bc": np.ascontiguousarray(np.broadcast_to(np.asarray(norm_final, np.float32)[None, :], (128, D))),
        "gmix_bc": np.ascontiguousarray(np.broadcast_to(np.asarray(norm_mix[0], np.float32)[None, :], (128, D))),
        "w_in": f(w_in[0]), "w_pool": f(w_pool[0]), "w_br_conv": f(w_br_conv[0]), "w_br_pool": f(w_br_pool[0]),
        "w_o": f(w_o[0]), "w_gate": f(w_gate[0]), "w_up": f(w_up[0]), "w_down": f(w_down[0]),
    }
    in_maps = []
    for c in range(8):
        xin = np.zeros((NPASS, WF, D), np.float32)
        sconv = np.zeros((NPASS, 2 * NSAMP, D), np.float32)
        spool = np.zeros((NPASS, 15 * NSAMP, D_POOL), np.float32)
        icnt = np.zeros((NPASS, 128, 64), np.float32)
        for p in range(NPASS):
            vs = 2 * c + p
            b, q = vs // 4, vs % 4
            if q > 0:
                xin[p, 0:HALO] = x_prompt[b, q * NPROMPT - HALO:q * NPROMPT]
            xin[p, HALO:HALO + NST] = x_sample[vs * NSAMP:(vs + 1) * NSAMP].reshape(NST, D)
            xin[p, HALO + NST:] = x_prompt[b, q * NPROMPT:(q + 1) * NPROMPT]
            sconv[p] = state_conv[0, vs * NSAMP:(vs + 1) * NSAMP].reshape(2 * NSAMP, D)
            spool[p] = state_pool[0, vs * NSAMP:(vs + 1) * NSAMP].reshape(15 * NSAMP, D_POOL)
            pos = q * NPROMPT + np.arange(16)
            for g, w in enumerate(POOL_W):
                icnt[p, :, 16 * g:16 * g + 16] = (1.0 / np.minimum(pos + 1, w)).astype(np.float32)[None, :]
        m = dict(shared)
        m.update({"xin": xin, "sconv": sconv, "spool": spool, "icnt": icnt})
        in_maps.append(m)
    return in_maps


def assemble(results):
    y_prompt = np.zeros((4, 2048, D), np.float32)
    y_sample = np.zeros((128, 4, D), np.float32)
    ncp = np.zeros((1, 4, 2, D), np.float32)
    npp = np.zeros((1, 4, 15, D_POOL), np.float32)
    ncs = np.zeros((1, 128, 2, D), np.float32)
    nps = np.zeros((1, 128, 15, D_POOL), np.float32)
    for c in range(8):
        r = results[c]
        for p in range(NPASS):
            vs = 2 * c + p
            b, q = vs // 4, vs % 4
            y_prompt[b, q * NPROMPT:(q + 1) * NPROMPT] = r["y_p"][p]
            y_sample[vs * NSAMP:(vs + 1) * NSAMP] = r["y_s"][p].reshape(NSAMP, 4, D)
            ncs[0, vs * NSAMP:(vs + 1) * NSAMP] = r["conv_o"][p][2:].reshape(NSAMP, 2, D)
            nps[0, vs * NSAMP:(vs + 1) * NSAMP] = r["pool_os"][p]
            if q == 3:
                ncp[0, b] = r["conv_o"][p][0:2]
                npp[0, b] = r["pool_op"][p]
    return (y_prompt, y_sample, ncp, npp, ncs, nps)


def kernel(x_prompt, x_sample, state_conv, state_pool, norm_mix, w_in, conv_w, w_pool, pool_scale,
           w_br_conv, w_br_pool, w_o, norm_ffn, w_gate, w_up, w_down, norm_final):
    in_maps = make_in_maps(x_prompt, x_sample, state_conv, state_pool, norm_mix, w_in, conv_w, w_pool,
                           pool_scale, w_br_conv, w_br_pool, w_o, norm_ffn, w_gate, w_up, w_down, norm_final)
    nc = build_program()
    res = run_bass_kernel_spmd(nc, in_maps, core_ids=list(range(8)))
    return assemble(res.results)
```

```python
import numpy as np
from contextlib import ExitStack

import concourse.bass as bass
import concourse.mybir as mybir
from concourse.bass_utils import run_bass_kernel_spmd

F32 = mybir.dt.float32
F32R = mybir.dt.float32r
BF16 = mybir.dt.bfloat16
AF = mybir.ActivationFunctionType
ALU = mybir.AluOpType

D = 2048
NCH = 16
D_POOL = 1024
D_FF = 5632
NFF = 44
D_IN = 11264
EPS = 1e-6
POOL_W = (2, 4, 8, 16)
NPASS = 2
HALO = 16
NPROMPT = 512
NSAMP = 8
NST = 32
WF = 560
WM = 544
TF = 280
TM = 272
SLOT_F = 4096
NSLOT = 6
NSCR = 8

G_MIX, G_FFN, G_FIN, G_CW, G_PS = 0, 16, 32, 48, 96
NGAIN = 104

MM_DT = BF16


class Res:
    __slots__ = ("name", "w", "r")

    def __init__(self, name):
        self.name = name
        self.w = None
        self.r = {}


class DSem:
    def __init__(self, key, h):
        self.key = key
        self.h = h
        self.val = 0


class Prog:
    ENG = ("pe", "act", "dve", "pool", "sp")

    def __init__(self, nc, stack):
        self.nc = nc
        self.stack = stack
        self.ops = {e: [] for e in self.ENG}
        self.semh = {}
        self.cnt = {e: 0 for e in self.ENG}
        self.known = {e: {} for e in self.ENG}
        self.snap = {}
        for e in self.ENG:
            self.semh[e] = stack.enter_context(nc.semaphore("sem_" + e))
        self.out_events = []
        self.n_dsem = 0
        self.sp_sems = [self.new_dsem() for _ in range(8)]
        self.sp_rr = 0

    def new_dsem(self):
        key = "d%d" % self.n_dsem
        self.n_dsem += 1
        h = self.stack.enter_context(self.nc.semaphore("sem_" + key))
        self.semh[key] = h
        return DSem(key, h)

    @staticmethod
    def _add(deps, ev):
        if ev is None:
            return
        k, v = ev
        if deps.get(k, 0) < v:
            deps[k] = v

    def _deps(self, reads, writes):
        deps = {}
        for r in reads:
            self._add(deps, r.w)
        for w in writes:
            self._add(deps, w.w)
            for k, v in w.r.items():
                self._add(deps, (k, v))
        return deps

    def _waits(self, eng, deps):
        kn = self.known[eng]
        for k, v in sorted(deps.items(), key=lambda kv: kv[0] == eng):
            if kn.get(k, 0) >= v:
                continue
            semh = self.semh[k]
            self.ops[eng].append(lambda e, s=semh, v=v: e.wait_ge(s, v))
            kn[k] = v
            sn = self.snap.get((k, v))
            if sn:
                for kk, vv in sn.items():
                    if kn.get(kk, 0) < vv:
                        kn[kk] = vv

    def _update(self, ev, reads, writes):
        for w in writes:
            w.w = ev
            w.r = {}
        for r in reads:
            if r in writes:
                continue
            k, v = ev
            if r.r.get(k, 0) < v:
                r.r[k] = v

    def emit(self, eng, fn, reads=(), writes=()):
        deps = self._deps(reads, writes)
        self._waits(eng, deps)
        self.cnt[eng] += 1
        v = self.cnt[eng]
        sem = self.semh[eng]
        self.ops[eng].append(lambda e, fn=fn, sem=sem: fn(e).then_inc(sem, 1))
        ev = (eng, v)
        self.snap[ev] = dict(self.known[eng])
        self._update(ev, reads, writes)
        return ev

    def dma(self, q, fn, reads=(), writes=(), dsem=None, is_output=False):
        if dsem is None:
            dsem = self.sp_sems[self.sp_rr % len(self.sp_sems)]
            self.sp_rr += 1
        deps = self._deps(reads, writes)
        if dsem.val > 0:
            self._add(deps, (dsem.key, dsem.val))
        self._waits(q, deps)
        dsem.val += 16
        v = dsem.val
        self.ops[q].append(lambda e, fn=fn, s=dsem.h: fn(e).then_inc(s, 16))
        ev = (dsem.key, v)
        self.snap[ev] = dict(self.known[q])
        self._update(ev, reads, writes)
        if is_output:
            self.out_events.append(ev)
        return ev

    def finish(self):
        deps = {}
        for ev in self.out_events:
            self._add(deps, ev)
        self._waits("sp", deps)


class Pool:
    def __init__(self, aps, name):
        self.aps = aps
        self.res = [Res("%s%d" % (name, i)) for i in range(len(aps))]
        self.free_list = list(range(len(aps)))

    def alloc(self):
        assert self.free_list, "scratch pool exhausted"
        return self.free_list.pop(0)

    def free(self, i):
        assert i not in self.free_list
        self.free_list.append(i)


def build_program(debug=None):
    nc = bass.Bass("TRN2", target_bir_lowering=False)

    def din(name, shape):
        return nc.dram_tensor(name, list(shape), F32, kind="ExternalInput").ap()

    def dout(name, shape):
        return nc.dram_tensor(name, list(shape), F32, kind="ExternalOutput").ap()

    xin = din("xin", [NPASS, WF, D])
    sconv = din("sconv", [NPASS, 2 * NSAMP, D])
    spool = din("spool", [NPASS, 15 * NSAMP, D_POOL])
    gains_d = din("gains", [128, NGAIN])
    icnt_d = din("icnt", [NPASS, 128, 64])
    ident_d = din("ident", [128, 128])
    gfin_d = din("gfin_bc", [128, D])
    gmix_d = din("gmix_bc", [128, D])
    w_in = din("w_in", [D, D_IN])
    w_pool = din("w_pool", [4, 256, 256])
    w_br_conv = din("w_br_conv", [D, D])
    w_br_pool = din("w_br_pool", [D_POOL, D])
    w_o = din("w_o", [D, D])
    w_gate = din("w_gate", [D, D_FF])
    w_up = din("w_up", [D, D_FF])
    w_down = din("w_down", [D_FF, D])

    y_p = dout("y_p", [NPASS, NPROMPT, D])
    y_s = dout("y_s", [NPASS, NST, D])
    conv_o = dout("conv_o", [NPASS, 2 + 2 * NSAMP, D])
    pool_op = dout("pool_op", [NPASS, 15, D_POOL])
    pool_os = dout("pool_os", [NPASS, NSAMP, 15, D_POOL])
    dbg_out = {}
    if debug:
        for name, shape in debug.items():
            dbg_out[name] = dout("dbg_" + name, shape)

    w_in_v = w_in.rearrange("(kc p) e -> p kc e", p=128)
    w_brc_v = w_br_conv.rearrange("(kc p) e -> p kc e", p=128)
    w_brp_v = w_br_pool.rearrange("(kc p) e -> p kc e", p=128)
    w_o_v = w_o.rearrange("(kc p) e -> p kc e", p=128)
    w_gate_v = w_gate.rearrange("(kc p) e -> p kc e", p=128)
    w_up_v = w_up.rearrange("(kc p) e -> p kc e", p=128)
    w_down_v = w_down.rearrange("(kc p) e -> p kc e", p=128)
    w_pool_v = w_pool.rearrange("g (kc p) e -> p (g kc) e", p=128)

    with ExitStack() as stack:
        def sb(name, shape, dt=F32):
            return stack.enter_context(nc.sbuf_tensor(name, list(shape), dt))

        RA = sb("RA", [128, NCH, WF], MM_DT)
        RB = sb("RB", [128, NCH, WM], MM_DT)
        RC = sb("RC", [128, NCH, WM], MM_DT)
        RM = sb("RM", [128, 8, WM], MM_DT)
        RP = sb("RP", [128, 4, WM], MM_DT)
        FX = sb("FX", [128, NCH, WF])
        XSB = sb("XSB", [128, 2, D])
        WPOOL = sb("WPOOL", [128, 8, 256], MM_DT)
        RING = sb("RING", [128, NSLOT, SLOT_F], MM_DT)
        SCR = sb("SCR", [128, NSCR, WF])
        UE = sb("UE", [128, NCH, NSAMP, 6])
        VE = sb("VE", [128, 8, NSAMP, 19])
        STU = sb("STU", [128, NCH, 18])
        STV = sb("STV", [128, 8, 47])
        GAINS = sb("GAINS", [128, NGAIN])
        ICNT = sb("ICNT", [128, 64])
        IDENT = sb("IDENT", [128, 128])
        ONES = sb("ONES", [128, 128])
        EPS_T = sb("EPS_T", [128, 1])
        SS = sb("SS", [128, 2, NSAMP, 19])
        T16 = sb("T16", [128, 16])
        SSQ = sb("SSQ", [128, 8])
        SSQ2 = sb("SSQ2", [128, 2, 4])
        GBC = sb("GBC", [128, D])
        IDENTB = sb("IDENTB", [128, 128], BF16)
        PS = [stack.enter_context(nc.psum_tensor("PS%d" % g, [128, 1024], F32)) for g in range(4)]

        P = Prog(nc, stack)
        slot_sems = [P.new_dsem() for _ in range(NSLOT)]
        misc_sem = P.new_dsem()

        rA = [Res("A%d" % j) for j in range(NCH)]
        rB = [Res("B%d" % j) for j in range(NCH)]
        rC = [Res("C%d" % j) for j in range(NCH)]
        rXS = [Res("XS0"), Res("XS1")]
        rM = [Res("M%d" % j) for j in range(8)]
        rP = [Res("P%d" % j) for j in range(4)]
        rF = [Res("F%d" % j) for j in range(NCH)]
        rSLOT = [Res("slot%d" % s) for s in range(NSLOT)]
        rPS = [Res("PS%d" % g) for g in range(4)]
        rUE = Res("UE")
        rVE = Res("VE")
        rSTU = Res("STU")
        rSTV = Res("STV")
        rCONST = Res("CONST")
        rGATE = Res("GATE")
        rWP = Res("WPOOL")
        rICNT = Res("ICNT")
        rSS = [Res("SS0"), Res("SS1")]
        rT16 = Res("T16")
        rSSQ = Res("SSQ")
        rSSQ2 = [Res("SSQ2a"), Res("SSQ2b")]
        rGBC = Res("GBC")
        scr = Pool([SCR[:, i, :] for i in range(NSCR)], "scr")

        A_r = RA[:]
        B_r = RB[:]
        C_r = RC[:]

        def XS(b):
            return XSB[:, b, :]

        def MIX_r(c):
            return RM[:, c, :]

        def split2(ap):
            return ap.rearrange("p (t c) -> p t c", t=2)

        def ps2(g, n):
            return PS[g][:, :].rearrange("p (t c) -> p t c", t=2)[:, :, 0:n]

        def gcol(c):
            return GAINS[:, c:c + 1]

        state = {"pg": 0, "slot": 0, "evac": 0}

        def next_pg():
            g = state["pg"] % 4
            state["pg"] += 1
            return g

        def dve(fn, reads, writes):
            return P.emit("dve", fn, reads, writes)

        def act(fn, reads, writes):
            return P.emit("act", fn, reads, writes)

        def copy_any(out, in_, reads, writes, same_engine=False):
            if not same_engine:
                state["evac"] += 1
            if state["evac"] % 2:
                return act(lambda e: e.activation(out, in_, AF.Copy), reads, writes)
            return dve(lambda e: e.tensor_copy(out, in_), reads, writes)

        def load_slab(dram_ap, kc, ncols):
            s = state["slot"] % NSLOT
            gated = (rGATE,) if 2 <= state["slot"] < NSLOT else ()
            state["slot"] += 1
            view = RING[:, s, 0:kc * ncols].rearrange("p (k c) -> p k c", k=kc)
            P.dma("pool", lambda e: e.dma_start(out=view, in_=dram_ap), reads=gated, writes=(rSLOT[s],),
                  dsem=slot_sems[s])
            return s, view

        def mm_group(klist, n, reads):
            g = next_pg()
            nk = len(klist)

            def fn(e):
                last = None
                for ki, (lhsT, rhs_fn) in enumerate(klist):
                    for t in range(2):
                        last = e.matmul(PS[g][:, 512 * t:512 * t + n], lhsT=lhsT, rhs=rhs_fn(t),
                                        start=(ki == 0), stop=(ki == nk - 1))
                return last
            P.emit("pe", fn, reads, (rPS[g],))
            return g

        def dbg_dump(name, ap_in, reads, idx=None):
            if debug and name in debug:
                o = dbg_out[name]
                if idx is not None:
                    o = o[idx]
                P.dma("sp", lambda e: e.dma_start(out=o, in_=ap_in), reads=reads, writes=(), is_output=True)

        rEPS = Res("EPS")
        dve(lambda e: e.memset(EPS_T[:], EPS), (), (rEPS,))
        dve(lambda e: e.memset(ONES[:], 1.0), (), (rEPS,))
        wp_sem = P.new_dsem()
        c_sems = [P.new_dsem() for _ in range(3)]

        def load_constants():
            P.dma("sp", lambda e: e.dma_start(out=GBC[:], in_=gmix_d[:, :]), writes=(rGBC,), dsem=c_sems[0])
            P.dma("sp", lambda e: e.dma_start(out=IDENT[:], in_=ident_d[:, :]), writes=(rCONST,), dsem=c_sems[1])
            dve(lambda e: e.tensor_copy(IDENTB[:], IDENT[:]), (rCONST,), (rCONST,))
            P.dma("pool", lambda e: e.dma_start(out=WPOOL[:], in_=w_pool_v), writes=(rWP,), dsem=wp_sem)

        def load_gains():
            P.dma("sp", lambda e: e.dma_start(out=GAINS[:], in_=gains_d[:, :]), writes=(rCONST,), dsem=c_sems[1])

        def transposes_in(src_b, rows, ncol_chunks, reads_extra=()):
            out = []
            for c0 in range(0, ncol_chunks, 8):
                ncc = min(8, ncol_chunks - c0)
                g = next_pg()

                def fn(e, c0=c0, ncc=ncc, g=g):
                    last = None
                    for c in range(ncc):
                        last = e.transpose(PS[g][:, c * 128:c * 128 + rows],
                                           XS(src_b)[0:rows, (c0 + c) * 128:(c0 + c + 1) * 128],
                                           IDENT[0:rows, 0:rows])
                    return last
                P.emit("pe", fn, (rXS[src_b], rCONST) + tuple(reads_extra), (rPS[g],))
                out.append((g, c0, ncc))
            return out

        def XNb(b):
            return RM[:, 4 * b:4 * b + 4, :].rearrange("p a c -> p (a c)")[:, 0:D]

        def tile_rows(ti):
            rows = 48 if ti == 0 else 128
            r0 = 0 if ti == 0 else 48 + 128 * (ti - 1)
            return rows, r0

        def tile_stage(ti, nstage):
            sidx = ti % nstage
            if sidx < 2:
                return XS(sidx), (rXS[sidx],)
            c0 = 4 * (sidx - 2)
            return FX[:, c0:c0 + 4, :].rearrange("p a c -> p (a c)")[:, 0:D], tuple(rF[c0:c0 + 4])

        def norm_tile_dma(pi, ti, nstage=2):
            rows, r0 = tile_rows(ti)
            xs_ap, xs_res = tile_stage(ti, nstage)
            gate = (rGATE,) if (nstage == 4 and ti == 4) else ()
            P.dma("sp", lambda e: e.dma_start(out=xs_ap[0:rows, :], in_=xin[pi, r0:r0 + rows, :]),
                  writes=xs_res + gate)

        def norm_tile_A(pi, ti, nstage=2):
            norm_tile_dma(pi, ti, nstage)
            norm_tile_cmp(pi, ti, nstage)

        def norm_tile_cmp(pi, ti, nstage=2):
            b = ti % 2
            rows, r0 = tile_rows(ti)
            rxn = tuple(rM[4 * b:4 * b + 4])
            xs_ap, xs_res = tile_stage(ti, nstage)
            act(lambda e: e.activation(XNb(b)[0:rows, :], xs_ap[0:rows, :], AF.Square,
                                       accum_out=SSQ2[0:rows, b, 0:1]),
                xs_res, rxn + (rSSQ2[b],))
            act(lambda e: e.activation(SSQ2[0:rows, b, 1:2], SSQ2[0:rows, b, 0:1], AF.Sqrt,
                                       bias=EPS_T[0:rows, 0:1], scale=1.0 / D),
                (rSSQ2[b], rEPS), (rSSQ2[b],))
            dve(lambda e: e.reciprocal(SSQ2[0:rows, b, 2:3], SSQ2[0:rows, b, 1:2]), (rSSQ2[b],), (rSSQ2[b],))
            dve(lambda e: e.scalar_tensor_tensor(XNb(b)[0:rows, :], xs_ap[0:rows, :], SSQ2[0:rows, b, 2:3],
                                                 GBC[0:rows, :], ALU.mult, ALU.mult),
                xs_res + (rSSQ2[b], rGBC), rxn)

        def norm_tile_B(pi, ti):
            b = ti % 2
            rows, r0 = tile_rows(ti)
            rxn = tuple(rM[4 * b:4 * b + 4])
            g = next_pg()
            psb = PS[g][:, :].bitcast(BF16)

            def fn(e):
                last = None
                for c in range(NCH):
                    last = e.transpose(psb[:, c * 128:c * 128 + rows], XNb(b)[0:rows, c * 128:(c + 1) * 128],
                                       IDENTB[0:rows, 0:rows])
                return last
            P.emit("pe", fn, rxn + (rCONST,), (rPS[g],))
            pv = psb.rearrange("p (c k) -> p c k", c=NCH)
            if ti == 0:
                copy_any(A_r[:, :, 0:HALO], pv[:, :, 0:HALO], (rPS[g],), tuple(rA))
                copy_any(A_r[:, :, HALO + NPROMPT:WF], pv[:, :, HALO:48], (rPS[g],), tuple(rA), same_engine=True)
            else:
                o = HALO + 128 * (ti - 1)
                copy_any(A_r[:, :, o:o + 128], pv[:, :, 0:128], (rPS[g],), tuple(rA))

        def s0_steps(pi, last, nstage=2):
            def st_dma():
                P.dma("sp", lambda e: e.dma_start(out=XS(0)[0:16, :], in_=sconv[pi, :, :]), writes=(rXS[0],))
                P.dma("sp", lambda e: e.dma_start(out=XS(1)[0:120, 0:D_POOL], in_=spool[pi, :, :]),
                      writes=(rXS[1],))
                P.dma("sp", lambda e: e.dma_start(
                    out=pool_os[pi, :, 0:11, :],
                    in_=spool[pi, :, :].rearrange("(s t) c -> s t c", t=15)[:, 4:15, :]), is_output=True)

            def st_tr():
                for (g, c0, ncc) in transposes_in(0, 16, NCH):
                    pv = PS[g][:, :].rearrange("p (c k) -> p c k", c=8)[:, 0:ncc, 0:16]
                    copy_any(UE[:, c0:c0 + ncc, :, 0:2], pv.rearrange("p c (s t) -> p c s t", t=2),
                             (rPS[g],), (rUE,))
                for (g, c0, ncc) in transposes_in(1, 120, 8):
                    pv = PS[g][:, :].rearrange("p (c k) -> p c k", c=8)[:, 0:ncc, 0:120]
                    copy_any(VE[:, c0:c0 + ncc, :, 0:15], pv.rearrange("p c (s t) -> p c s t", t=15),
                             (rPS[g],), (rVE,))

            def fin():
                norm_tile_B(pi, 4)
                if last:
                    P.dma("sp", lambda e: e.dma_start(out=GBC[:], in_=gfin_d[:, :]), writes=(rGBC,),
                          dsem=misc_sem)
            return [
                st_dma,
                st_tr,
                lambda: (norm_tile_A(pi, 0, nstage), norm_tile_A(pi, 1, nstage)),
                lambda: (norm_tile_B(pi, 0), norm_tile_A(pi, 2, nstage)),
                lambda: (norm_tile_B(pi, 1), norm_tile_A(pi, 3, nstage)),
                lambda: (norm_tile_B(pi, 2), norm_tile_A(pi, 4, nstage)),
                lambda: norm_tile_B(pi, 3),
                fin,
            ]

        def raw_dma(pi, ti):
            b = (ti + 1) % 2
            rows, r0 = tile_rows(ti)
            P.dma("sp", lambda e: e.dma_start(out=XS(b)[0:rows, :], in_=xin[pi, r0:r0 + rows, :]),
                  writes=(rXS[b],))

        def load_raw_tile(pi, ti):
            b = (ti + 1) % 2
            rows, r0 = tile_rows(ti)
            for (g, c0, ncc) in transposes_in(b, rows, NCH):
                pv = PS[g][:, :].rearrange("p (c k) -> p c k", c=8)
                wr = tuple(rF[c0:c0 + ncc])
                if ti == 0:
                    copy_any(FX[:, c0:c0 + ncc, HALO + NPROMPT:WF], pv[:, 0:ncc, HALO:48], (rPS[g],), wr)
                else:
                    o = HALO + 128 * (ti - 1)
                    copy_any(FX[:, c0:c0 + ncc, o:o + 128], pv[:, 0:ncc, 0:128], (rPS[g],), wr)
            if ti + 2 < 5:
                raw_dma(pi, ti + 2)

        class Norm:
            def __init__(self, src_fn, src_res, width, gain0, dst_fn, dst_res):
                self.src_fn, self.src_res, self.width = src_fn, src_res, width
                self.gain0, self.dst_fn, self.dst_res = gain0, dst_fn, dst_res
                self.acc = None

            def add(self, j):
                width = self.width
                if self.acc is None:
                    self.acc = scr.alloc()
                    acc_ap = scr.aps[self.acc][:, 0:width]
                    act(lambda e: e.activation(acc_ap, self.src_fn(j), AF.Square),
                        (self.src_res[j],), (scr.res[self.acc],))
                    return
                acc_ap = scr.aps[self.acc][:, 0:width]
                sq = scr.alloc()
                sq_ap = scr.aps[sq][:, 0:width]
                act(lambda e: e.activation(sq_ap, self.src_fn(j), AF.Square), (self.src_res[j],), (scr.res[sq],))
                dve(lambda e: e.tensor_tensor(acc_ap, acc_ap, sq_ap, ALU.add),
                    (scr.res[sq], scr.res[self.acc]), (scr.res[self.acc],))
                scr.free(sq)

            def finish(self):
                rb = self.finish_stats()
                rb_ap = scr.aps[rb][:, 0:self.width]
                for j in range(NCH):
                    rd = (self.src_res[j], scr.res[rb], rCONST)
                    wr = (self.dst_res[j],)
                    dve(lambda e, j=j: e.scalar_tensor_tensor(self.dst_fn(j), self.src_fn(j),
                                                              gcol(self.gain0 + j), rb_ap, ALU.mult, ALU.mult),
                        rd, wr)
                scr.free(rb)

            def finish_stats(self):
                width = self.width
                tw = width // 2
                acc = self.acc
                acc_ap = scr.aps[acc][:, 0:width]
                g = next_pg()

                def fn(e):
                    last = None
                    for t in range(2):
                        last = e.matmul(PS[g][:, 512 * t:512 * t + tw], lhsT=ONES[:, :],
                                        rhs=acc_ap[:, tw * t:tw * (t + 1)], start=True, stop=True)
                    return last
                P.emit("pe", fn, (scr.res[acc], rEPS), (rPS[g],))
                rt = scr.alloc()
                rt_ap = scr.aps[rt][:, 0:width]
                act(lambda e: e.activation(split2(rt_ap), ps2(g, tw), AF.Sqrt, bias=EPS_T[:, 0:1], scale=1.0 / D),
                    (rPS[g], rEPS), (scr.res[rt],))
                scr.free(acc)
                rb = scr.alloc()
                rb_ap = scr.aps[rb][:, 0:width]
                dve(lambda e: e.reciprocal(rb_ap, rt_ap), (scr.res[rt],), (scr.res[rb],))
                scr.free(rt)
                return rb

        def emit_s8_tile(pi, tt):
            ncols = 128 if tt < 4 else NST
            c0 = 128 * tt
            b = tt % 2
            gs = []
            for half in range(2):
                g = next_pg()

                def fn(e, half=half, g=g, ncols=ncols, c0=c0):
                    last = None
                    for c in range(8):
                        last = e.transpose(PS[g][0:ncols, c * 128:(c + 1) * 128],
                                           FX[:, 8 * half + c, HALO + c0:HALO + c0 + ncols], IDENT[:, :])
                    return last
                P.emit("pe", fn, tuple(rF[8 * half:8 * half + 8]) + (rCONST,), (rPS[g],))
                act(lambda e, g=g, half=half, ncols=ncols, b=b: e.activation(
                    XS(b)[0:ncols, 1024 * half:1024 * (half + 1)], PS[g][0:ncols, :], AF.Square,
                    accum_out=SSQ[0:ncols, half:half + 1]),
                    (rPS[g],), (rXS[b], rSSQ))
                gs.append(g)
            dve(lambda e, ncols=ncols: e.tensor_tensor(SSQ[0:ncols, 2:3], SSQ[0:ncols, 0:1], SSQ[0:ncols, 1:2],
                                                       ALU.add), (rSSQ,), (rSSQ,))
            act(lambda e, ncols=ncols: e.activation(SSQ[0:ncols, 3:4], SSQ[0:ncols, 2:3], AF.Sqrt,
                                                    bias=EPS_T[0:ncols, 0:1], scale=1.0 / D),
                (rSSQ, rEPS), (rSSQ,))
            dve(lambda e, ncols=ncols: e.reciprocal(SSQ[0:ncols, 4:5], SSQ[0:ncols, 3:4]), (rSSQ,), (rSSQ,))
            for half in range(2):
                g = gs[half]
                dve(lambda e, g=g, half=half, ncols=ncols, b=b: e.scalar_tensor_tensor(
                    XS(b)[0:ncols, 1024 * half:1024 * (half + 1)], PS[g][0:ncols, :], SSQ[0:ncols, 4:5],
                    GBC[0:ncols, 1024 * half:1024 * (half + 1)], ALU.mult, ALU.mult),
                    (rPS[g], rSSQ, rGBC), (rXS[b],))
            if tt < 4:
                P.dma("sp", lambda e, pi=pi, b=b, c0=c0: e.dma_start(out=y_p[pi, c0:c0 + 128, :],
                                                                     in_=XS(b)[0:128, :]),
                      reads=(rXS[b],), is_output=True)
            else:
                P.dma("sp", lambda e, pi=pi, b=b: e.dma_start(out=y_s[pi, :, :], in_=XS(b)[0:NST, :]),
                      reads=(rXS[b],), is_output=True)


        s8_pending = [None]
        for pi in range(NPASS):
            late_steps = []
            if pi == 0:
                st = s0_steps(0, NPASS == 1, nstage=4)
                norm_tile_dma(0, 0, 4)
                norm_tile_dma(0, 1, 4)
                load_constants()
                norm_tile_dma(0, 2, 4)
                norm_tile_dma(0, 3, 4)
                load_gains()
                norm_tile_cmp(0, 0, 4)
                norm_tile_cmp(0, 1, 4)
                norm_tile_B(0, 0)
                norm_tile_dma(0, 4, 4)
                norm_tile_cmp(0, 2, 4)
                norm_tile_B(0, 1)
                norm_tile_cmp(0, 3, 4)
                norm_tile_B(0, 2)
                norm_tile_cmp(0, 4, 4)
                norm_tile_B(0, 3)
                st[7]()
                st[0]()
                late_steps = [st[1]]
            P.dma("sp", lambda e, pi=pi: e.dma_start(out=ICNT[:], in_=icnt_d[pi, :, :]), writes=(rICNT,),
                  dsem=misc_sem)

            def A_rhs_full(k):
                return lambda t: A_r[:, k, TF * t:TF * (t + 1)]

            def A_rhs_main(k):
                return lambda t: A_r[:, k, HALO + TM * t:HALO + TM * (t + 1)]

            for jp in range(NCH // 2):
                js = (2 * jp, 2 * jp + 1)
                if s8_pending[0] is not None and jp < 5:
                    emit_s8_tile(s8_pending[0], jp)
                    if jp == 4:
                        s8_pending[0] = None
                if jp == NCH // 2 - 1:
                    raw_dma(pi, 0)
                sH, vH = load_slab(w_in_v[:, :, 256 * jp:256 * jp + 256], NCH, 256)
                hs = []
                for jj, j in enumerate(js):
                    g = mm_group([(vH[:, k, 128 * jj:128 * jj + 128], A_rhs_full(k)) for k in range(NCH)], TF,
                                 tuple(rA) + (rSLOT[sH],))
                    t = scr.alloc()
                    act(lambda e, t=t, g=g: e.activation(split2(scr.aps[t][:, 0:WF]), ps2(g, TF), AF.Copy),
                        (rPS[g],), (scr.res[t],))
                    hs.append(t)
                while late_steps:
                    late_steps.pop(0)()
                sC, vC = load_slab(w_in_v[:, :, 4096 + 256 * jp:4096 + 256 * jp + 256], NCH, 256)
                cvs = []
                for jj, j in enumerate(js):
                    g = mm_group([(vC[:, k, 128 * jj:128 * jj + 128], A_rhs_full(k)) for k in range(NCH)], TF,
                                 tuple(rA) + (rSLOT[sC],))
                    u = scr.alloc()
                    u_ap = scr.aps[u]
                    h_ap = scr.aps[hs[jj]]
                    dve(lambda e, u_ap=u_ap, h_ap=h_ap, g=g: e.tensor_tensor(
                        split2(u_ap[:, 0:WF]), ps2(g, TF), split2(h_ap[:, 0:WF]), ALU.mult),
                        (rPS[g], scr.res[hs[jj]]), (scr.res[u],))
                    scr.free(hs[jj])
                    act(lambda e, u_ap=u_ap, j=j: e.activation(
                        UE[:, j, :, 2:6], u_ap[:, HALO + NPROMPT:WF].rearrange("p (s t) -> p s t", t=4), AF.Copy),
                        (scr.res[u],), (rUE,))
                    act(lambda e, u_ap=u_ap, j=j: e.activation(
                        STU[:, j, 0:2], u_ap[:, HALO + NPROMPT - 2:HALO + NPROMPT], AF.Copy),
                        (scr.res[u],), (rSTU,))
                    cv = scr.alloc()
                    cv_ap = scr.aps[cv]
                    cw = [gcol(G_CW + 16 * k + j) for k in range(3)]
                    p0, p1 = HALO, HALO + NPROMPT
                    dve(lambda e, u_ap=u_ap, cv_ap=cv_ap, cw=cw: e.tensor_scalar(
                        cv_ap[:, p0:p1], u_ap[:, p0 - 2:p1 - 2], cw[0], None, ALU.mult),
                        (scr.res[u], rCONST), (scr.res[cv],))
                    dve(lambda e, u_ap=u_ap, cv_ap=cv_ap, cw=cw: e.scalar_tensor_tensor(
                        cv_ap[:, p0:p1], u_ap[:, p0 - 1:p1 - 1], cw[1], cv_ap[:, p0:p1], ALU.mult, ALU.add),
                        (scr.res[u], rCONST, scr.res[cv]), (scr.res[cv],))
                    dve(lambda e, u_ap=u_ap, cv_ap=cv_ap, cw=cw: e.scalar_tensor_tensor(
                        cv_ap[:, p0:p1], u_ap[:, p0:p1], cw[2], cv_ap[:, p0:p1], ALU.mult, ALU.add),
                        (scr.res[u], rCONST, scr.res[cv]), (scr.res[cv],))
                    scr.free(u)
                    cvs_ap = cv_ap[:, p1:WF].rearrange("p (s t) -> p s t", t=4)
                    dve(lambda e, cvs_ap=cvs_ap, cw=cw, j=j: e.tensor_scalar(
                        cvs_ap, UE[:, j, :, 0:4], cw[0], None, ALU.mult),
                        (rUE, rCONST, scr.res[cv]), (scr.res[cv],))
                    dve(lambda e, cvs_ap=cvs_ap, cw=cw, j=j: e.scalar_tensor_tensor(
                        cvs_ap, UE[:, j, :, 1:5], cw[1], cvs_ap, ALU.mult, ALU.add),
                        (rUE, rCONST, scr.res[cv]), (scr.res[cv],))
                    dve(lambda e, cvs_ap=cvs_ap, cw=cw, j=j: e.scalar_tensor_tensor(
                        cvs_ap, UE[:, j, :, 2:6], cw[2], cvs_ap, ALU.mult, ALU.add),
                        (rUE, rCONST, scr.res[cv]), (scr.res[cv],))
                    cvs.append(cv)
                sB, vB = load_slab(w_in_v[:, :, 2048 + 256 * jp:2048 + 256 * jp + 256], NCH, 256)
                for jj, j in enumerate(js):
                    g = mm_group([(vB[:, k, 128 * jj:128 * jj + 128], A_rhs_main(k)) for k in range(NCH)], TM,
                                 tuple(rA) + (rSLOT[sB],))
                    cv_ap = scr.aps[cvs[jj]]
                    dve(lambda e, cv_ap=cv_ap, g=g, j=j: e.tensor_tensor(
                        split2(B_r[:, j, :]), ps2(g, TM), split2(cv_ap[:, HALO:WF]), ALU.mult),
                        (rPS[g], scr.res[cvs[jj]]), (rB[j],))
                    scr.free(cvs[jj])
            act(lambda e: e.activation(STU[:, :, 2:18].rearrange("p c (s t) -> p c s t", t=2), UE[:, :, :, 4:6],
                                       AF.Copy), (rUE,), (rSTU,))
            for half in range(2):
                g = next_pg()

                def fn(e, half=half, g=g):
                    last = None
                    for c in range(8):
                        last = e.transpose(PS[g][0:18, c * 128:(c + 1) * 128], STU[:, 8 * half + c, :],
                                           IDENT[:, :])
                    return last
                P.emit("pe", fn, (rSTU, rCONST), (rPS[g],))
                copy_any(XS(0)[0:18, 1024 * half:1024 * (half + 1)], PS[g][0:18, :], (rPS[g],), (rXS[0],))
            P.dma("sp", lambda e, pi=pi: e.dma_start(out=conv_o[pi, :, :], in_=XS(0)[0:18, :]),
                  reads=(rXS[0],), is_output=True)
            raw_dma(pi, 1)

            vWP = WPOOL[:]

            def emit_s3(grp):
                pls = [(grp % 2) * 2, (grp % 2) * 2 + 1]
                for ee in range(2):
                    c = 2 * grp + ee
                    klist = []
                    for kc in range(2):
                        pl_r = RP[:, pls[kc], :]
                        klist.append((vWP[:, 2 * grp + kc, 128 * ee:128 * ee + 128],
                                      (lambda t, pl_r=pl_r: pl_r[:, TM * t:TM * (t + 1)])))
                    g = mm_group(klist, TM, (rP[pls[0]], rP[pls[1]], rWP))
                    act(lambda e, g=g, c=c: e.activation(split2(MIX_r(c)), ps2(g, TM), AF.Copy,
                                                         scale=gcol(G_PS + c)),
                        (rPS[g], rCONST), (rM[c],))
            for jp in range(4):
                grp = jp
                w = POOL_W[grp]
                nstep = {2: 1, 4: 2, 8: 3, 16: 4}[w]
                sV, vV = load_slab(w_in_v[:, :, 6144 + 256 * jp:6144 + 256 * jp + 256], NCH, 256)
                pls = []
                for jj in range(2):
                    j = 2 * jp + jj
                    g = mm_group([(vV[:, k, 128 * jj:128 * jj + 128], A_rhs_full(k)) for k in range(NCH)], TF,
                                 tuple(rA) + (rSLOT[sV],))
                    vv = scr.alloc()
                    vv_ap = scr.aps[vv]
                    act(lambda e, vv_ap=vv_ap, g=g: e.activation(split2(vv_ap[:, 0:WF]), ps2(g, TF), AF.Copy),
                        (rPS[g],), (scr.res[vv],))
                    act(lambda e, vv_ap=vv_ap, j=j: e.activation(
                        VE[:, j, :, 15:19], vv_ap[:, HALO + NPROMPT:WF].rearrange("p (s t) -> p s t", t=4), AF.Copy),
                        (scr.res[vv],), (rVE,))
                    act(lambda e, vv_ap=vv_ap, j=j: e.activation(
                        STV[:, j, 0:15], vv_ap[:, HALO + NPROMPT - 15:HALO + NPROMPT], AF.Copy),
                        (scr.res[vv],), (rSTV,))
                    PE_ = HALO + NPROMPT
                    cur = vv
                    tmps = []
                    for st in range(nstep):
                        sh = 1 << st
                        lo = (1 << (st + 1)) - 1
                        nx = scr.alloc()
                        dve(lambda e, a=scr.aps[cur], o=scr.aps[nx], sh=sh, lo=lo: e.tensor_tensor(
                            o[:, lo:PE_], a[:, lo:PE_], a[:, lo - sh:PE_ - sh], ALU.add),
                            (scr.res[cur],), (scr.res[nx],))
                        if cur != vv:
                            scr.free(cur)
                        cur = nx
                    pl = (jp % 2) * 2 + jj
                    pl_r = RP[:, pl, :]
                    dve(lambda e, s_ap=scr.aps[cur], vv_ap=vv_ap, pl_r=pl_r, w=w: e.scalar_tensor_tensor(
                        pl_r[:, 0:NPROMPT], s_ap[:, HALO:PE_], 1.0 / w, vv_ap[:, HALO:PE_], ALU.mult, ALU.subtract),
                        (scr.res[cur], scr.res[vv]), (rP[pl],))
                    dve(lambda e, s_ap=scr.aps[cur], grp=grp: e.tensor_tensor(
                        T16[:, :], s_ap[:, HALO:HALO + 16], ICNT[:, 16 * grp:16 * grp + 16], ALU.mult),
                        (scr.res[cur], rICNT), (rT16,))
                    dve(lambda e, vv_ap=vv_ap, pl_r=pl_r: e.tensor_tensor(
                        pl_r[:, 0:16], T16[:, :], vv_ap[:, HALO:HALO + 16], ALU.subtract),
                        (rT16, scr.res[vv], rP[pl]), (rP[pl],))
                    scr.free(cur)
                    curs = None
                    for st in range(nstep):
                        sh = 1 << st
                        lo = (1 << (st + 1)) - 1
                        src_ap = VE[:, j, :, :] if curs is None else SS[:, curs, :, :]
                        src_res = rVE if curs is None else rSS[curs]
                        nxs = 0 if curs is None else 1 - curs
                        dve(lambda e, a=src_ap, o=SS[:, nxs, :, :], sh=sh, lo=lo: e.tensor_tensor(
                            o[:, :, lo:19], a[:, :, lo:19], a[:, :, lo - sh:19 - sh], ALU.add),
                            (src_res,), (rSS[nxs],))
                        curs = nxs
                    dve(lambda e, curs=curs, pl_r=pl_r, w=w, j=j: e.scalar_tensor_tensor(
                        pl_r[:, NPROMPT:WM].rearrange("p (s t) -> p s t", t=4), SS[:, curs, :, 15:19], 1.0 / w,
                        VE[:, j, :, 15:19], ALU.mult, ALU.subtract),
                        (rSS[curs], rVE, rP[pl]), (rP[pl],))
                    scr.free(vv)
                    pls.append(pl)
                if jp >= 1:
                    emit_s3(jp - 1)
                load_raw_tile(pi, jp)
            act(lambda e: e.activation(STV[:, :, 15:47].rearrange("p c (s t) -> p c s t", t=4), VE[:, :, :, 15:19],
                                       AF.Copy), (rVE,), (rSTV,))

            def B_rhs(k):
                return lambda t: B_r[:, k, TM * t:TM * (t + 1)]

            def C_rhs(k):
                return lambda t: C_r[:, k, TM * t:TM * (t + 1)]

            def MIX_rhs(k):
                return lambda t: MIX_r(k)[:, TM * t:TM * (t + 1)]

            for jp in range(NCH // 2):
                js = (2 * jp, 2 * jp + 1)
                sG, vG = load_slab(w_in_v[:, :, 7168 + 256 * jp:7168 + 256 * jp + 256], NCH, 256)
                sgc = []
                for jj, j in enumerate(js):
                    g = mm_group([(vG[:, k, 128 * jj:128 * jj + 128], A_rhs_main(k)) for k in range(NCH)], TM,
                                 tuple(rA) + (rSLOT[sG],))
                    t = scr.alloc()
                    act(lambda e, t=t, g=g: e.activation(split2(scr.aps[t][:, 0:WM]), ps2(g, TM), AF.Sigmoid),
                        (rPS[g],), (scr.res[t],))
                    sgc.append(t)
                sY, vY = load_slab(w_brc_v[:, :, 256 * jp:256 * jp + 256], NCH, 256)
                m1 = []
                for jj, j in enumerate(js):
                    g = mm_group([(vY[:, k, 128 * jj:128 * jj + 128], B_rhs(k)) for k in range(NCH)], TM,
                                 tuple(rB) + (rSLOT[sY],))
                    t = scr.alloc()
                    dve(lambda e, t=t, g=g, s=sgc[jj]: e.tensor_tensor(
                        split2(scr.aps[t][:, 0:WM]), ps2(g, TM), split2(scr.aps[s][:, 0:WM]), ALU.mult),
                        (rPS[g], scr.res[sgc[jj]]), (scr.res[t],))
                    scr.free(sgc[jj])
                    m1.append(t)
                sG2, vG2 = load_slab(w_in_v[:, :, 9216 + 256 * jp:9216 + 256 * jp + 256], NCH, 256)
                sgp = []
                for jj, j in enumerate(js):
                    g = mm_group([(vG2[:, k, 128 * jj:128 * jj + 128], A_rhs_main(k)) for k in range(NCH)], TM,
                                 tuple(rA) + (rSLOT[sG2],))
                    t = scr.alloc()
                    act(lambda e, t=t, g=g: e.activation(split2(scr.aps[t][:, 0:WM]), ps2(g, TM), AF.Sigmoid),
                        (rPS[g],), (scr.res[t],))
                    sgp.append(t)
                if jp == 0:
                    emit_s3(3)
                    load_raw_tile(pi, 4)
                sY2, vY2 = load_slab(w_brp_v[:, :, 256 * jp:256 * jp + 256], 8, 256)
                for jj, j in enumerate(js):
                    g = mm_group([(vY2[:, k, 128 * jj:128 * jj + 128], MIX_rhs(k)) for k in range(8)], TM,
                                 tuple(rM) + (rSLOT[sY2],))
                    t = scr.alloc()
                    dve(lambda e, t=t, g=g, s=sgp[jj]: e.tensor_tensor(
                        split2(scr.aps[t][:, 0:WM]), ps2(g, TM), split2(scr.aps[s][:, 0:WM]), ALU.mult),
                        (rPS[g], scr.res[sgp[jj]]), (scr.res[t],))
                    scr.free(sgp[jj])
                    dve(lambda e, t=t, m=m1[jj], j=j: e.tensor_tensor(
                        C_r[:, j, :], scr.aps[t][:, 0:WM], scr.aps[m][:, 0:WM], ALU.add),
                        (scr.res[t], scr.res[m1[jj]]), (rC[j],))
                    scr.free(t)
                    scr.free(m1[jj])

            g = next_pg()

            def fn(e, g=g):
                last = None
                for c in range(8):
                    last = e.transpose(PS[g][0:47, c * 128:(c + 1) * 128], STV[:, c, :], IDENT[:, :])
                return last
            P.emit("pe", fn, (rSTV, rCONST), (rPS[g],))
            copy_any(XS(1)[0:47, 0:D_POOL], PS[g][0:47, :], (rPS[g],), (rXS[1],))
            P.dma("sp", lambda e, pi=pi: e.dma_start(out=pool_op[pi, :, :], in_=XS(1)[0:15, 0:D_POOL]),
                  reads=(rXS[1],), is_output=True)
            for s in range(NSAMP):
                P.dma("sp", lambda e, pi=pi, s=s: e.dma_start(out=pool_os[pi, s, 11:15, :],
                                                              in_=XS(1)[15 + 4 * s:19 + 4 * s, 0:D_POOL]),
                      reads=(rXS[1],), is_output=True)

            nrm = Norm(lambda j: FX[:, j, HALO:WF], rF, WM, G_FFN, lambda j: B_r[:, j, :], rB)
            for jp in range(NCH // 2):
                sO, vO = load_slab(w_o_v[:, :, 256 * jp:256 * jp + 256], NCH, 256)
                for jj in range(2):
                    j = 2 * jp + jj
                    g = mm_group([(vO[:, k, 128 * jj:128 * jj + 128], C_rhs(k)) for k in range(NCH)], TM,
                                 tuple(rC) + (rSLOT[sO],))
                    dve(lambda e, g=g, j=j: e.tensor_tensor(
                        split2(FX[:, j, HALO:WF]), ps2(g, TM), split2(FX[:, j, HALO:WF]), ALU.add),
                        (rPS[g], rF[j]), (rF[j],))
                    act(lambda e, j=j: e.activation(B_r[:, j, :], FX[:, j, HALO:WF], AF.Copy,
                                                    scale=gcol(G_FFN + j)),
                        (rF[j], rCONST), (rB[j],))
                    nrm.add(j)
            if debug and "x1" in debug:
                for j in range(NCH):
                    dbg_dump("x1", FX[:, j, HALO:WF], (rF[j],), idx=(pi, slice(None), j))

            rb6_h = [None]

            def HN_rhs(k):
                return lambda t: B_r[:, k, TM * t:TM * (t + 1)]

            nblk = (NFF + 7) // 8
            def emit_down(fb):
                nf = min(8, NFF - 8 * fb)
                base = (fb % 2) * 8
                for jq in range(4):
                    sD, vD = load_slab(w_down_v[:, 8 * fb:8 * fb + nf, 512 * jq:512 * jq + 512], nf, 512)
                    for jj in range(4):
                        j = 4 * jq + jj
                        g = mm_group([(vD[:, k, 128 * jj:128 * jj + 128], C_rhs(base + k)) for k in range(nf)], TM,
                                     tuple(rC[base:base + nf]) + (rSLOT[sD],))
                        dve(lambda e, g=g, j=j: e.tensor_tensor(
                            split2(FX[:, j, HALO:WF]), ps2(g, TM), split2(FX[:, j, HALO:WF]), ALU.add),
                            (rPS[g], rF[j]), (rF[j],))

            nxt_steps = s0_steps(pi + 1, pi + 2 == NPASS) if pi + 1 < NPASS else []
            pair_no = 0
            for fb in range(nblk):
                nf = min(8, NFF - 8 * fb)
                base = (fb % 2) * 8
                for fp in range(nf // 2):
                    if pair_no % 2 == 1 and nxt_steps:
                        nxt_steps.pop(0)()
                    pair_no += 1
                    col = 1024 * fb + 256 * fp
                    sG, vG = load_slab(w_gate_v[:, :, col:col + 256], NCH, 256)
                    sl = []
                    ggs = []
                    for ff in range(2):
                        ggs.append(mm_group([(vG[:, k, 128 * ff:128 * ff + 128], HN_rhs(k)) for k in range(NCH)],
                                            TM, tuple(rB) + (rSLOT[sG],)))
                    if rb6_h[0] is None:
                        rb6_h[0] = nrm.finish_stats()
                    rb6 = rb6_h[0]
                    rb6_ap = scr.aps[rb6][:, 0:WM]
                    for ff in range(2):
                        g = ggs[ff]
                        t = scr.alloc()
                        t_ap = scr.aps[t][:, 0:WM]
                        dve(lambda e, t_ap=t_ap, g=g, rb6_ap=rb6_ap: e.tensor_tensor(
                            split2(t_ap), ps2(g, TM), split2(rb6_ap), ALU.mult),
                            (rPS[g], scr.res[rb6]), (scr.res[t],))
                        act(lambda e, t_ap=t_ap: e.activation(t_ap, t_ap, AF.Silu), (scr.res[t],), (scr.res[t],))
                        dve(lambda e, t_ap=t_ap, rb6_ap=rb6_ap: e.tensor_tensor(t_ap, t_ap, rb6_ap, ALU.mult),
                            (scr.res[t], scr.res[rb6]), (scr.res[t],))
                        sl.append(t)
                    sU, vU = load_slab(w_up_v[:, :, col:col + 256], NCH, 256)
                    for ff in range(2):
                        slot = base + 2 * fp + ff
                        g = mm_group([(vU[:, k, 128 * ff:128 * ff + 128], HN_rhs(k)) for k in range(NCH)], TM,
                                     tuple(rB) + (rSLOT[sU],))
                        dve(lambda e, g=g, t=sl[ff], slot=slot: e.tensor_tensor(
                            split2(C_r[:, slot, :]), ps2(g, TM), split2(scr.aps[t][:, 0:WM]), ALU.mult),
                            (rPS[g], scr.res[sl[ff]]), (rC[slot],))
                        scr.free(sl[ff])
                if fb >= 1:
                    emit_down(fb - 1)
            emit_down(nblk - 1)
            while nxt_steps:
                nxt_steps.pop(0)()

            if debug and "x2" in debug:
                for j in range(NCH):
                    dbg_dump("x2", FX[:, j, HALO:WF], (rF[j],), idx=(pi, slice(None), j))
            scr.free(rb6_h[0])
            if pi == NPASS - 1:
                for tt in range(5):
                    emit_s8_tile(pi, tt)
            else:
                s8_pending[0] = pi

        P.finish()

        with nc.Block() as block:
            @block.tensor
            def _(e):
                for op in P.ops["pe"]:
                    op(e)

            @block.scalar
            def _(e):
                for op in P.ops["act"]:
                    op(e)

            @block.vector
            def _(e):
                for op in P.ops["dve"]:
                    op(e)

            @block.gpsimd
            def _(e):
                for op in P.ops["pool"]:
                    op(e)

            @block.sync
            def _(e):
                for op in P.ops["sp"]:
                    op(e)
    return nc


def _fm(v):
    return np.ascontiguousarray(np.asarray(v, np.float32).reshape(-1, 128).T)


def make_in_maps(x_prompt, x_sample, state_conv, state_pool, norm_mix, w_in, conv_w, w_pool, pool_scale,
                 w_br_conv, w_br_pool, w_o, norm_ffn, w_gate, w_up, w_down, norm_final):
    f = lambda a: np.ascontiguousarray(np.asarray(a, np.float32))
    x_prompt, x_sample = f(x_prompt), f(x_sample)
    state_conv, state_pool = f(state_conv), f(state_pool)
    gains = np.concatenate([_fm(norm_mix[0]), _fm(norm_ffn[0]), _fm(norm_final)] +
                           [_fm(conv_w[0][k]) for k in range(3)] + [_fm(pool_scale[0])], axis=1)
    gains = np.ascontiguousarray(gains, np.float32)
    assert gains.shape == (128, NGAIN)
    ident = np.eye(128, dtype=np.float32)
    shared = {
        "gains": gains, "ident": ident,
        "gfin_bc": np.ascontiguousarray(np.broadcast_to(np.asarray(norm_final, np.float32)[None, :], (128, D))),
        "gmix_bc": np.ascontiguousarray(np.broadcast_to(np.asarray(norm_mix[0], np.float32)[None, :], (128, D))),
        "w_in": f(w_in[0]), "w_pool": f(w_pool[0]), "w_br_conv": f(w_br_conv[0]), "w_br_pool": f(w_br_pool[0]),
        "w_o": f(w_o[0]), "w_gate": f(w_gate[0]), "w_up": f(w_up[0]), "w_down": f(w_down[0]),
    }
    in_maps = []
    for c in range(8):
        xin = np.zeros((NPASS, WF, D), np.float32)
        sconv = np.zeros((NPASS, 2 * NSAMP, D), np.float32)
        spool = np.zeros((NPASS, 15 * NSAMP, D_POOL), np.float32)
        icnt = np.zeros((NPASS, 128, 64), np.float32)
        for p in range(NPASS):
            vs = 2 * c + p
            b, q = vs // 4, vs % 4
            if q > 0:
                xin[p, 0:HALO] = x_prompt[b, q * NPROMPT - HALO:q * NPROMPT]
            xin[p, HALO:HALO + NST] = x_sample[vs * NSAMP:(vs + 1) * NSAMP].reshape(NST, D)
            xin[p, HALO + NST:] = x_prompt[b, q * NPROMPT:(q + 1) * NPROMPT]
            sconv[p] = state_conv[0, vs * NSAMP:(vs + 1) * NSAMP].reshape(2 * NSAMP, D)
            spool[p] = state_pool[0, vs * NSAMP:(vs + 1) * NSAMP].reshape(15 * NSAMP, D_POOL)
            pos = q * NPROMPT + np.arange(16)
            for g, w in enumerate(POOL_W):
                icnt[p, :, 16 * g:16 * g + 16] = (1.0 / np.minimum(pos + 1, w)).astype(np.float32)[None, :]
        m = dict(shared)
        m.update({"xin": xin, "sconv": sconv, "spool": spool, "icnt": icnt})
        in_maps.append(m)
    return in_maps


def assemble(results):
    y_prompt = np.zeros((4, 2048, D), np.float32)
    y_sample = np.zeros((128, 4, D), np.float32)
    ncp = np.zeros((1, 4, 2, D), np.float32)
    npp = np.zeros((1, 4, 15, D_POOL), np.float32)
    ncs = np.zeros((1, 128, 2, D), np.float32)
    nps = np.zeros((1, 128, 15, D_POOL), np.float32)
    for c in range(8):
        r = results[c]
        for p in range(NPASS):
            vs = 2 * c + p
            b, q = vs // 4, vs % 4
            y_prompt[b, q * NPROMPT:(q + 1) * NPROMPT] = r["y_p"][p]
            y_sample[vs * NSAMP:(vs + 1) * NSAMP] = r["y_s"][p].reshape(NSAMP, 4, D)
            ncs[0, vs * NSAMP:(vs + 1) * NSAMP] = r["conv_o"][p][2:].reshape(NSAMP, 2, D)
            nps[0, vs * NSAMP:(vs + 1) * NSAMP] = r["pool_os"][p]
            if q == 3:
                ncp[0, b] = r["conv_o"][p][0:2]
                npp[0, b] = r["pool_op"][p]
    return (y_prompt, y_sample, ncp, npp, ncs, nps)


def kernel(x_prompt, x_sample, state_conv, state_pool, norm_mix, w_in, conv_w, w_pool, pool_scale,
           w_br_conv, w_br_pool, w_o, norm_ffn, w_gate, w_up, w_down, norm_final):
    in_maps = make_in_maps(x_prompt, x_sample, state_conv, state_pool, norm_mix, w_in, conv_w, w_pool,
                           pool_scale, w_br_conv, w_br_pool, w_o, norm_ffn, w_gate, w_up, w_down, norm_final)
    nc = build_program()
    res = run_bass_kernel_spmd(nc, in_maps, core_ids=list(range(8)))
    return assemble(res.results)
```

```python
import numpy as np
from contextlib import ExitStack

import concourse.bass as bass
import concourse.mybir as mybir
from concourse.bass_utils import run_bass_kernel_spmd

F32 = mybir.dt.float32
F32R = mybir.dt.float32r
BF16 = mybir.dt.bfloat16
AF = mybir.ActivationFunctionType
ALU = mybir.AluOpType

D = 2048
NCH = 16
D_POOL = 1024
D_FF = 5632
NFF = 44
D_IN = 11264
EPS = 1e-6
POOL_W = (2, 4, 8, 16)
NPASS = 2
HALO = 16
NPROMPT = 512
NSAMP = 8
NST = 32
WF = 560
WM = 544
TF = 280
TM = 272
SLOT_F = 4096
NSLOT = 6
NSCR = 8

G_MIX, G_FFN, G_FIN, G_CW, G_PS = 0, 16, 32, 48, 96
NGAIN = 104

MM_DT = BF16


class Res:
    __slots__ = ("name", "w", "r")

    def __init__(self, name):
        self.name = name
        self.w = None
        self.r = {}


class DSem:
    def __init__(self, key, h):
        self.key = key
        self.h = h
        self.val = 0


class Prog:
    ENG = ("pe", "act", "dve", "pool", "sp")

    def __init__(self, nc, stack):
        self.nc = nc
        self.stack = stack
        self.ops = {e: [] for e in self.ENG}
        self.semh = {}
        self.cnt = {e: 0 for e in self.ENG}
        self.known = {e: {} for e in self.ENG}
        self.snap = {}
        for e in self.ENG:
            self.semh[e] = stack.enter_context(nc.semaphore("sem_" + e))
        self.out_events = []
        self.n_dsem = 0
        self.sp_sems = [self.new_dsem() for _ in range(8)]
        self.sp_rr = 0

    def new_dsem(self):
        key = "d%d" % self.n_dsem
        self.n_dsem += 1
        h = self.stack.enter_context(self.nc.semaphore("sem_" + key))
        self.semh[key] = h
        return DSem(key, h)

    @staticmethod
    def _add(deps, ev):
        if ev is None:
            return
        k, v = ev
        if deps.get(k, 0) < v:
            deps[k] = v

    def _deps(self, reads, writes):
        deps = {}
        for r in reads:
            self._add(deps, r.w)
        for w in writes:
            self._add(deps, w.w)
            for k, v in w.r.items():
                self._add(deps, (k, v))
        return deps

    def _waits(self, eng, deps):
        kn = self.known[eng]
        for k, v in sorted(deps.items(), key=lambda kv: kv[0] == eng):
            if kn.get(k, 0) >= v:
                continue
            semh = self.semh[k]
            self.ops[eng].append(lambda e, s=semh, v=v: e.wait_ge(s, v))
            kn[k] = v
            sn = self.snap.get((k, v))
            if sn:
                for kk, vv in sn.items():
                    if kn.get(kk, 0) < vv:
                        kn[kk] = vv

    def _update(self, ev, reads, writes):
        for w in writes:
            w.w = ev
            w.r = {}
        for r in reads:
            if r in writes:
                continue
            k, v = ev
            if r.r.get(k, 0) < v:
                r.r[k] = v

    def emit(self, eng, fn, reads=(), writes=()):
        deps = self._deps(reads, writes)
        self._waits(eng, deps)
        self.cnt[eng] += 1
        v = self.cnt[eng]
        sem = self.semh[eng]
        self.ops[eng].append(lambda e, fn=fn, sem=sem: fn(e).then_inc(sem, 1))
        ev = (eng, v)
        self.snap[ev] = dict(self.known[eng])
        self._update(ev, reads, writes)
        return ev

    def dma(self, q, fn, reads=(), writes=(), dsem=None, is_output=False):
        if dsem is None:
            dsem = self.sp_sems[self.sp_rr % len(self.sp_sems)]
            self.sp_rr += 1
        deps = self._deps(reads, writes)
        if dsem.val > 0:
            self._add(deps, (dsem.key, dsem.val))
        self._waits(q, deps)
        dsem.val += 16
        v = dsem.val
        self.ops[q].append(lambda e, fn=fn, s=dsem.h: fn(e).then_inc(s, 16))
        ev = (dsem.key, v)
        self.snap[ev] = dict(self.known[q])
        self._update(ev, reads, writes)
        if is_output:
            self.out_events.append(ev)
        return ev

    def finish(self):
        deps = {}
        for ev in self.out_events:
            self._add(deps, ev)
        self._waits("sp", deps)


class Pool:
    def __init__(self, aps, name):
        self.aps = aps
        self.res = [Res("%s%d" % (name, i)) for i in range(len(aps))]
        self.free_list = list(range(len(aps)))

    def alloc(self):
        assert self.free_list, "scratch pool exhausted"
        return self.free_list.pop(0)

    def free(self, i):
        assert i not in self.free_list
        self.free_list.append(i)


def build_program(debug=None):
    nc = bass.Bass("TRN2", target_bir_lowering=False)

    def din(name, shape):
        return nc.dram_tensor(name, list(shape), F32, kind="ExternalInput").ap()

    def dout(name, shape):
        return nc.dram_tensor(name, list(shape), F32, kind="ExternalOutput").ap()

    xin = din("xin", [NPASS, WF, D])
    sconv = din("sconv", [NPASS, 2 * NSAMP, D])
    spool = din("spool", [NPASS, 15 * NSAMP, D_POOL])
    gains_d = din("gains", [128, NGAIN])
    icnt_d = din("icnt", [NPASS, 128, 64])
    ident_d = din("ident", [128, 128])
    gfin_d = din("gfin_bc", [128, D])
    gmix_d = din("gmix_bc", [128, D])
    w_in = din("w_in", [D, D_IN])
    w_pool = din("w_pool", [4, 256, 256])
    w_br_conv = din("w_br_conv", [D, D])
    w_br_pool = din("w_br_pool", [D_POOL, D])
    w_o = din("w_o", [D, D])
    w_gate = din("w_gate", [D, D_FF])
    w_up = din("w_up", [D, D_FF])
    w_down = din("w_down", [D_FF, D])

    y_p = dout("y_p", [NPASS, NPROMPT, D])
    y_s = dout("y_s", [NPASS, NST, D])
    conv_o = dout("conv_o", [NPASS, 2 + 2 * NSAMP, D])
    pool_op = dout("pool_op", [NPASS, 15, D_POOL])
    pool_os = dout("pool_os", [NPASS, NSAMP, 15, D_POOL])
    dbg_out = {}
    if debug:
        for name, shape in debug.items():
            dbg_out[name] = dout("dbg_" + name, shape)

    w_in_v = w_in.rearrange("(kc p) e -> p kc e", p=128)
    w_brc_v = w_br_conv.rearrange("(kc p) e -> p kc e", p=128)
    w_brp_v = w_br_pool.rearrange("(kc p) e -> p kc e", p=128)
    w_o_v = w_o.rearrange("(kc p) e -> p kc e", p=128)
    w_gate_v = w_gate.rearrange("(kc p) e -> p kc e", p=128)
    w_up_v = w_up.rearrange("(kc p) e -> p kc e", p=128)
    w_down_v = w_down.rearrange("(kc p) e -> p kc e", p=128)
    w_pool_v = w_pool.rearrange("g (kc p) e -> p (g kc) e", p=128)

    with ExitStack() as stack:
        def sb(name, shape, dt=F32):
            return stack.enter_context(nc.sbuf_tensor(name, list(shape), dt))

        RA = sb("RA", [128, NCH, WF], MM_DT)
        RB = sb("RB", [128, NCH, WM], MM_DT)
        RC = sb("RC", [128, NCH, WM], MM_DT)
        RM = sb("RM", [128, 8, WM], MM_DT)
        RP = sb("RP", [128, 4, WM], MM_DT)
        FX = sb("FX", [128, NCH, WF])
        XSB = sb("XSB", [128, 2, D])
        WPOOL = sb("WPOOL", [128, 8, 256], MM_DT)
        RING = sb("RING", [128, NSLOT, SLOT_F], MM_DT)
        SCR = sb("SCR", [128, NSCR, WF])
        UE = sb("UE", [128, NCH, NSAMP, 6])
        VE = sb("VE", [128, 8, NSAMP, 19])
        STU = sb("STU", [128, NCH, 18])
        STV = sb("STV", [128, 8, 47])
        GAINS = sb("GAINS", [128, NGAIN])
        ICNT = sb("ICNT", [128, 64])
        IDENT = sb("IDENT", [128, 128])
        ONES = sb("ONES", [128, 128])
        EPS_T = sb("EPS_T", [128, 1])
        SS = sb("SS", [128, 2, NSAMP, 19])
        T16 = sb("T16", [128, 16])
        SSQ = sb("SSQ", [128, 8])
        SSQ2 = sb("SSQ2", [128, 2, 4])
        GBC = sb("GBC", [128, D])
        IDENTB = sb("IDENTB", [128, 128], BF16)
        PS = [stack.enter_context(nc.psum_tensor("PS%d" % g, [128, 1024], F32)) for g in range(4)]

        P = Prog(nc, stack)
        slot_sems = [P.new_dsem() for _ in range(NSLOT)]
        misc_sem = P.new_dsem()

        rA = [Res("A%d" % j) for j in range(NCH)]
        rB = [Res("B%d" % j) for j in range(NCH)]
        rC = [Res("C%d" % j) for j in range(NCH)]
        rXS = [Res("XS0"), Res("XS1")]
        rM = [Res("M%d" % j) for j in range(8)]
        rP = [Res("P%d" % j) for j in range(4)]
        rF = [Res("F%d" % j) for j in range(NCH)]
        rSLOT = [Res("slot%d" % s) for s in range(NSLOT)]
        rPS = [Res("PS%d" % g) for g in range(4)]
        rUE = Res("UE")
        rVE = Res("VE")
        rSTU = Res("STU")
        rSTV = Res("STV")
        rCONST = Res("CONST")
        rID = Res("IDENT")
        rGATE = Res("GATE")
        rWP = Res("WPOOL")
        rICNT = Res("ICNT")
        rSS = [Res("SS0"), Res("SS1")]
        rT16 = Res("T16")
        rSSQ = Res("SSQ")
        rSSQ2 = [Res("SSQ2a"), Res("SSQ2b")]
        rGBC = Res("GBC")
        scr = Pool([SCR[:, i, :] for i in range(NSCR)], "scr")

        A_r = RA[:]
        B_r = RB[:]
        C_r = RC[:]

        def XS(b):
            return XSB[:, b, :]

        def MIX_r(c):
            return RM[:, c, :]

        def split2(ap):
            return ap.rearrange("p (t c) -> p t c", t=2)

        def ps2(g, n):
            return PS[g][:, :].rearrange("p (t c) -> p t c", t=2)[:, :, 0:n]

        def gcol(c):
            return GAINS[:, c:c + 1]

        state = {"pg": 0, "slot": 0, "evac": 0}

        def next_pg():
            g = state["pg"] % 4
            state["pg"] += 1
            return g

        def dve(fn, reads, writes):
            return P.emit("dve", fn, reads, writes)

        def act(fn, reads, writes):
            return P.emit("act", fn, reads, writes)

        def copy_any(out, in_, reads, writes, same_engine=False):
            if not same_engine:
                state["evac"] += 1
            if state["evac"] % 2:
                return act(lambda e: e.activation(out, in_, AF.Copy), reads, writes)
            return dve(lambda e: e.tensor_copy(out, in_), reads, writes)

        def load_slab(dram_ap, kc, ncols):
            s = state["slot"] % NSLOT
            gated = (rGATE,) if 2 <= state["slot"] < NSLOT else ()
            state["slot"] += 1
            view = RING[:, s, 0:kc * ncols].rearrange("p (k c) -> p k c", k=kc)
            P.dma("pool", lambda e: e.dma_start(out=view, in_=dram_ap), reads=gated, writes=(rSLOT[s],),
                  dsem=slot_sems[s])
            return s, view

        def mm_group(klist, n, reads):
            g = next_pg()
            nk = len(klist)

            def fn(e):
                last = None
                for ki, (lhsT, rhs_fn) in enumerate(klist):
                    for t in range(2):
                        last = e.matmul(PS[g][:, 512 * t:512 * t + n], lhsT=lhsT, rhs=rhs_fn(t),
                                        start=(ki == 0), stop=(ki == nk - 1))
                return last
            P.emit("pe", fn, reads, (rPS[g],))
            return g

        def dbg_dump(name, ap_in, reads, idx=None):
            if debug and name in debug:
                o = dbg_out[name]
                if idx is not None:
                    o = o[idx]
                P.dma("sp", lambda e: e.dma_start(out=o, in_=ap_in), reads=reads, writes=(), is_output=True)

        rEPS = Res("EPS")
        dve(lambda e: e.memset(EPS_T[:], EPS), (), (rEPS,))
        dve(lambda e: e.memset(ONES[:], 1.0), (), (rEPS,))
        act(lambda e: e.activation(SSQ[:, 6:7], EPS_T[:, 0:1], AF.Square), (rEPS,), (rSSQ,))
        wp_sem = P.new_dsem()
        c_sems = [P.new_dsem() for _ in range(3)]

        def load_constants():
            P.dma("sp", lambda e: e.dma_start(out=GBC[:], in_=gmix_d[:, :]), writes=(rGBC,), dsem=c_sems[0])
            P.dma("sp", lambda e: e.dma_start(out=IDENT[:], in_=ident_d[:, :]), writes=(rID,), dsem=c_sems[1])
            dve(lambda e: e.tensor_copy(IDENTB[:], IDENT[:]), (rID,), (rID,))
            P.dma("pool", lambda e: e.dma_start(out=WPOOL[:], in_=w_pool_v), writes=(rWP,), dsem=wp_sem)

        def load_gains():
            P.dma("sp", lambda e: e.dma_start(out=GAINS[:], in_=gains_d[:, :]), writes=(rCONST,), dsem=c_sems[2])

        def transposes_in(src_b, rows, ncol_chunks, reads_extra=()):
            out = []
            for c0 in range(0, ncol_chunks, 8):
                ncc = min(8, ncol_chunks - c0)
                g = next_pg()

                def fn(e, c0=c0, ncc=ncc, g=g):
                    last = None
                    for c in range(ncc):
                        last = e.transpose(PS[g][:, c * 128:c * 128 + rows],
                                           XS(src_b)[0:rows, (c0 + c) * 128:(c0 + c + 1) * 128],
                                           IDENT[0:rows, 0:rows])
                    return last
                P.emit("pe", fn, (rXS[src_b], rID) + tuple(reads_extra), (rPS[g],))
                out.append((g, c0, ncc))
            return out

        def XNb(b):
            return RM[:, 4 * b:4 * b + 4, :].rearrange("p a c -> p (a c)")[:, 0:D]

        def tile_rows(ti):
            rows = 48 if ti == 0 else 128
            r0 = 0 if ti == 0 else 48 + 128 * (ti - 1)
            return rows, r0

        def tile_stage(ti, nstage):
            sidx = ti % nstage
            if sidx < 2:
                return XS(sidx), (rXS[sidx],)
            c0 = 4 * (sidx - 2)
            return FX[:, c0:c0 + 4, :].rearrange("p a c -> p (a c)")[:, 0:D], tuple(rF[c0:c0 + 4])

        def norm_tile_dma(pi, ti, nstage=2):
            rows, r0 = tile_rows(ti)
            xs_ap, xs_res = tile_stage(ti, nstage)
            gate = (rGATE,) if (nstage == 4 and ti == 4) else ()
            P.dma("sp", lambda e: e.dma_start(out=xs_ap[0:rows, :], in_=xin[pi, r0:r0 + rows, :]),
                  writes=xs_res + gate)

        def norm_tile_A(pi, ti, nstage=2):
            norm_tile_dma(pi, ti, nstage)
            norm_tile_cmp(pi, ti, nstage)

        def norm_tile_cmp(pi, ti, nstage=2):
            b = ti % 2
            rows, r0 = tile_rows(ti)
            rxn = tuple(rM[4 * b:4 * b + 4])
            xs_ap, xs_res = tile_stage(ti, nstage)
            act(lambda e: e.activation(XNb(b)[0:rows, :], xs_ap[0:rows, :], AF.Square,
                                       accum_out=SSQ2[0:rows, b, 0:1]),
                xs_res, rxn + (rSSQ2[b],))
            act(lambda e: e.activation(SSQ2[0:rows, b, 1:2], SSQ2[0:rows, b, 0:1], AF.Sqrt,
                                       bias=EPS_T[0:rows, 0:1], scale=1.0 / D),
                (rSSQ2[b], rEPS), (rSSQ2[b],))
            dve(lambda e: e.reciprocal(SSQ2[0:rows, b, 2:3], SSQ2[0:rows, b, 1:2]), (rSSQ2[b],), (rSSQ2[b],))
            dve(lambda e: e.scalar_tensor_tensor(XNb(b)[0:rows, :], xs_ap[0:rows, :], SSQ2[0:rows, b, 2:3],
                                                 GBC[0:rows, :], ALU.mult, ALU.mult),
                xs_res + (rSSQ2[b], rGBC), rxn)

        def norm_tile_B(pi, ti):
            b = ti % 2
            rows, r0 = tile_rows(ti)
            rxn = tuple(rM[4 * b:4 * b + 4])
            g = next_pg()
            psb = PS[g][:, :].bitcast(BF16)

            def fn(e):
                last = None
                for c in range(NCH):
                    last = e.transpose(psb[:, c * 128:c * 128 + rows], XNb(b)[0:rows, c * 128:(c + 1) * 128],
                                       IDENTB[0:rows, 0:rows])
                return last
            P.emit("pe", fn, rxn + (rID,), (rPS[g],))
            pv = psb.rearrange("p (c k) -> p c k", c=NCH)
            if ti == 0:
                copy_any(A_r[:, :, 0:HALO], pv[:, :, 0:HALO], (rPS[g],), tuple(rA))
                copy_any(A_r[:, :, HALO + NPROMPT:WF], pv[:, :, HALO:48], (rPS[g],), tuple(rA), same_engine=True)
            else:
                o = HALO + 128 * (ti - 1)
                copy_any(A_r[:, :, o:o + 128], pv[:, :, 0:128], (rPS[g],), tuple(rA))

        def s0_steps(pi, last, nstage=2):
            def st_dma():
                P.dma("sp", lambda e: e.dma_start(out=XS(0)[0:16, :], in_=sconv[pi, :, :]), writes=(rXS[0],))
                P.dma("sp", lambda e: e.dma_start(out=XS(1)[0:120, 0:D_POOL], in_=spool[pi, :, :]),
                      writes=(rXS[1],))
                P.dma("sp", lambda e: e.dma_start(
                    out=pool_os[pi, :, 0:11, :],
                    in_=spool[pi, :, :].rearrange("(s t) c -> s t c", t=15)[:, 4:15, :]), is_output=True)

            def st_tr():
                for (g, c0, ncc) in transposes_in(0, 16, NCH):
                    pv = PS[g][:, :].rearrange("p (c k) -> p c k", c=8)[:, 0:ncc, 0:16]
                    copy_any(UE[:, c0:c0 + ncc, :, 0:2], pv.rearrange("p c (s t) -> p c s t", t=2),
                             (rPS[g],), (rUE,))
                for (g, c0, ncc) in transposes_in(1, 120, 8):
                    pv = PS[g][:, :].rearrange("p (c k) -> p c k", c=8)[:, 0:ncc, 0:120]
                    copy_any(VE[:, c0:c0 + ncc, :, 0:15], pv.rearrange("p c (s t) -> p c s t", t=15),
                             (rPS[g],), (rVE,))

            def fin():
                norm_tile_B(pi, 4)
                if last:
                    P.dma("sp", lambda e: e.dma_start(out=GBC[:], in_=gfin_d[:, :]), writes=(rGBC,),
                          dsem=misc_sem)
            return [
                st_dma,
                st_tr,
                lambda: (norm_tile_A(pi, 0, nstage), norm_tile_A(pi, 1, nstage)),
                lambda: (norm_tile_B(pi, 0), norm_tile_A(pi, 2, nstage)),
                lambda: (norm_tile_B(pi, 1), norm_tile_A(pi, 3, nstage)),
                lambda: (norm_tile_B(pi, 2), norm_tile_A(pi, 4, nstage)),
                lambda: norm_tile_B(pi, 3),
                fin,
            ]

        def raw_dma(pi, ti):
            b = (ti + 1) % 2
            rows, r0 = tile_rows(ti)
            P.dma("sp", lambda e: e.dma_start(out=XS(b)[0:rows, :], in_=xin[pi, r0:r0 + rows, :]),
                  writes=(rXS[b],))

        def load_raw_tile(pi, ti):
            b = (ti + 1) % 2
            rows, r0 = tile_rows(ti)
            for (g, c0, ncc) in transposes_in(b, rows, NCH):
                pv = PS[g][:, :].rearrange("p (c k) -> p c k", c=8)
                wr = tuple(rF[c0:c0 + ncc])
                if ti == 0:
                    copy_any(FX[:, c0:c0 + ncc, HALO + NPROMPT:WF], pv[:, 0:ncc, HALO:48], (rPS[g],), wr)
                else:
                    o = HALO + 128 * (ti - 1)
                    copy_any(FX[:, c0:c0 + ncc, o:o + 128], pv[:, 0:ncc, 0:128], (rPS[g],), wr)
            if ti + 2 < 5:
                raw_dma(pi, ti + 2)

        class Norm:
            def __init__(self, src_fn, src_res, width, gain0, dst_fn, dst_res):
                self.src_fn, self.src_res, self.width = src_fn, src_res, width
                self.gain0, self.dst_fn, self.dst_res = gain0, dst_fn, dst_res
                self.acc = None

            def add(self, j):
                width = self.width
                if self.acc is None:
                    self.acc = scr.alloc()
                    acc_ap = scr.aps[self.acc][:, 0:width]
                    act(lambda e: e.activation(acc_ap, self.src_fn(j), AF.Square),
                        (self.src_res[j],), (scr.res[self.acc],))
                    return
                acc_ap = scr.aps[self.acc][:, 0:width]
                sq = scr.alloc()
                sq_ap = scr.aps[sq][:, 0:width]
                act(lambda e: e.activation(sq_ap, self.src_fn(j), AF.Square), (self.src_res[j],), (scr.res[sq],))
                dve(lambda e: e.tensor_tensor(acc_ap, acc_ap, sq_ap, ALU.add),
                    (scr.res[sq], scr.res[self.acc]), (scr.res[self.acc],))
                scr.free(sq)

            def finish(self):
                rb = self.finish_stats()
                rb_ap = scr.aps[rb][:, 0:self.width]
                for j in range(NCH):
                    rd = (self.src_res[j], scr.res[rb], rCONST)
                    wr = (self.dst_res[j],)
                    dve(lambda e, j=j: e.scalar_tensor_tensor(self.dst_fn(j), self.src_fn(j),
                                                              gcol(self.gain0 + j), rb_ap, ALU.mult, ALU.mult),
                        rd, wr)
                scr.free(rb)

            def finish_stats(self):
                width = self.width
                tw = width // 2
                acc = self.acc
                acc_ap = scr.aps[acc][:, 0:width]
                g = next_pg()

                def fn(e):
                    last = None
                    for t in range(2):
                        last = e.matmul(PS[g][:, 512 * t:512 * t + tw], lhsT=ONES[:, :],
                                        rhs=acc_ap[:, tw * t:tw * (t + 1)], start=True, stop=True)
                    return last
                P.emit("pe", fn, (scr.res[acc], rEPS), (rPS[g],))
                rt = scr.alloc()
                rt_ap = scr.aps[rt][:, 0:width]
                act(lambda e: e.activation(split2(rt_ap), ps2(g, tw), AF.Sqrt, bias=EPS_T[:, 0:1], scale=1.0 / D),
                    (rPS[g], rEPS), (scr.res[rt],))
                scr.free(acc)
                rb = scr.alloc()
                rb_ap = scr.aps[rb][:, 0:width]
                dve(lambda e: e.reciprocal(rb_ap, rt_ap), (scr.res[rt],), (scr.res[rb],))
                scr.free(rt)
                return rb

        def emit_s8_tile(pi, tt):
            ncols = 128 if tt < 4 else NST
            c0 = 128 * tt
            b = tt % 2
            gs = []
            for half in range(2):
                g = next_pg()

                def fn(e, half=half, g=g, ncols=ncols, c0=c0):
                    last = None
                    for c in range(8):
                        last = e.transpose(PS[g][0:ncols, c * 128:(c + 1) * 128],
                                           FX[:, 8 * half + c, HALO + c0:HALO + c0 + ncols], IDENT[:, :])
                    return last
                P.emit("pe", fn, tuple(rF[8 * half:8 * half + 8]) + (rID,), (rPS[g],))
                act(lambda e, g=g, half=half, ncols=ncols, b=b: e.activation(
                    XS(b)[0:ncols, 1024 * half:1024 * (half + 1)], PS[g][0:ncols, :], AF.Square,
                    accum_out=SSQ[0:ncols, half:half + 1]),
                    (rPS[g],), (rXS[b], rSSQ))
                gs.append(g)
            dve(lambda e, ncols=ncols: e.tensor_tensor(SSQ[0:ncols, 2:3], SSQ[0:ncols, 0:1], SSQ[0:ncols, 1:2],
                                                       ALU.add), (rSSQ,), (rSSQ,))
            act(lambda e, ncols=ncols: e.activation(SSQ[0:ncols, 3:4], SSQ[0:ncols, 2:3], AF.Sqrt,
                                                    bias=EPS_T[0:ncols, 0:1], scale=1.0 / D),
                (rSSQ, rEPS), (rSSQ,))
            dve(lambda e, ncols=ncols: e.reciprocal(SSQ[0:ncols, 4:5], SSQ[0:ncols, 3:4]), (rSSQ,), (rSSQ,))
            for half in range(2):
                g = gs[half]
                dve(lambda e, g=g, half=half, ncols=ncols, b=b: e.scalar_tensor_tensor(
                    XS(b)[0:ncols, 1024 * half:1024 * (half + 1)], PS[g][0:ncols, :], SSQ[0:ncols, 4:5],
                    GBC[0:ncols, 1024 * half:1024 * (half + 1)], ALU.mult, ALU.mult),
                    (rPS[g], rSSQ, rGBC), (rXS[b],))
            if tt < 4:
                P.dma("sp", lambda e, pi=pi, b=b, c0=c0: e.dma_start(out=y_p[pi, c0:c0 + 128, :],
                                                                     in_=XS(b)[0:128, :]),
                      reads=(rXS[b],), is_output=True)
            else:
                P.dma("sp", lambda e, pi=pi, b=b: e.dma_start(out=y_s[pi, :, :], in_=XS(b)[0:NST, :]),
                      reads=(rXS[b],), is_output=True)


        s8_pending = [None]
        for pi in range(NPASS):
            late_steps = []
            if pi == 0:
                st = s0_steps(0, NPASS == 1, nstage=4)
                norm_tile_dma(0, 0, 4)
                norm_tile_dma(0, 1, 4)
                load_constants()
                norm_tile_dma(0, 2, 4)
                norm_tile_dma(0, 3, 4)
                load_gains()
                norm_tile_cmp(0, 0, 4)
                norm_tile_cmp(0, 1, 4)
                norm_tile_B(0, 0)
                norm_tile_dma(0, 4, 4)
                norm_tile_cmp(0, 2, 4)
                norm_tile_B(0, 1)
                norm_tile_cmp(0, 3, 4)
                norm_tile_B(0, 2)
                norm_tile_cmp(0, 4, 4)
                norm_tile_B(0, 3)
                st[7]()
                st[0]()
                late_steps = [st[1]]
            P.dma("sp", lambda e, pi=pi: e.dma_start(out=ICNT[:], in_=icnt_d[pi, :, :]), writes=(rICNT,),
                  dsem=misc_sem)

            def A_rhs_full(k):
                return lambda t: A_r[:, k, TF * t:TF * (t + 1)]

            def A_rhs_main(k):
                return lambda t: A_r[:, k, HALO + TM * t:HALO + TM * (t + 1)]

            for jp in range(NCH // 2):
                js = (2 * jp, 2 * jp + 1)
                if s8_pending[0] is not None and jp < 5:
                    emit_s8_tile(s8_pending[0], jp)
                    if jp == 4:
                        s8_pending[0] = None
                if jp == NCH // 2 - 1:
                    raw_dma(pi, 0)
                sH, vH = load_slab(w_in_v[:, :, 256 * jp:256 * jp + 256], NCH, 256)
                hs = []
                for jj, j in enumerate(js):
                    g = mm_group([(vH[:, k, 128 * jj:128 * jj + 128], A_rhs_full(k)) for k in range(NCH)], TF,
                                 tuple(rA) + (rSLOT[sH],))
                    t = scr.alloc()
                    act(lambda e, t=t, g=g: e.activation(split2(scr.aps[t][:, 0:WF]), ps2(g, TF), AF.Copy),
                        (rPS[g],), (scr.res[t],))
                    hs.append(t)
                while late_steps:
                    late_steps.pop(0)()
                sC, vC = load_slab(w_in_v[:, :, 4096 + 256 * jp:4096 + 256 * jp + 256], NCH, 256)
                cvs = []
                for jj, j in enumerate(js):
                    g = mm_group([(vC[:, k, 128 * jj:128 * jj + 128], A_rhs_full(k)) for k in range(NCH)], TF,
                                 tuple(rA) + (rSLOT[sC],))
                    u = scr.alloc()
                    u_ap = scr.aps[u]
                    h_ap = scr.aps[hs[jj]]
                    dve(lambda e, u_ap=u_ap, h_ap=h_ap, g=g: e.tensor_tensor(
                        split2(u_ap[:, 0:WF]), ps2(g, TF), split2(h_ap[:, 0:WF]), ALU.mult),
                        (rPS[g], scr.res[hs[jj]]), (scr.res[u],))
                    scr.free(hs[jj])
                    act(lambda e, u_ap=u_ap, j=j: e.activation(
                        UE[:, j, :, 2:6], u_ap[:, HALO + NPROMPT:WF].rearrange("p (s t) -> p s t", t=4), AF.Copy),
                        (scr.res[u],), (rUE,))
                    act(lambda e, u_ap=u_ap, j=j: e.activation(
                        STU[:, j, 0:2], u_ap[:, HALO + NPROMPT - 2:HALO + NPROMPT], AF.Copy),
                        (scr.res[u],), (rSTU,))
                    cv = scr.alloc()
                    cv_ap = scr.aps[cv]
                    cw = [gcol(G_CW + 16 * k + j) for k in range(3)]
                    p0, p1 = HALO, HALO + NPROMPT
                    dve(lambda e, u_ap=u_ap, cv_ap=cv_ap, cw=cw: e.tensor_scalar(
                        cv_ap[:, p0:p1], u_ap[:, p0 - 2:p1 - 2], cw[0], None, ALU.mult),
                        (scr.res[u], rCONST), (scr.res[cv],))
                    dve(lambda e, u_ap=u_ap, cv_ap=cv_ap, cw=cw: e.scalar_tensor_tensor(
                        cv_ap[:, p0:p1], u_ap[:, p0 - 1:p1 - 1], cw[1], cv_ap[:, p0:p1], ALU.mult, ALU.add),
                        (scr.res[u], rCONST, scr.res[cv]), (scr.res[cv],))
                    dve(lambda e, u_ap=u_ap, cv_ap=cv_ap, cw=cw: e.scalar_tensor_tensor(
                        cv_ap[:, p0:p1], u_ap[:, p0:p1], cw[2], cv_ap[:, p0:p1], ALU.mult, ALU.add),
                        (scr.res[u], rCONST, scr.res[cv]), (scr.res[cv],))
                    scr.free(u)
                    cvs_ap = cv_ap[:, p1:WF].rearrange("p (s t) -> p s t", t=4)
                    dve(lambda e, cvs_ap=cvs_ap, cw=cw, j=j: e.tensor_scalar(
                        cvs_ap, UE[:, j, :, 0:4], cw[0], None, ALU.mult),
                        (rUE, rCONST, scr.res[cv]), (scr.res[cv],))
                    dve(lambda e, cvs_ap=cvs_ap, cw=cw, j=j: e.scalar_tensor_tensor(
                        cvs_ap, UE[:, j, :, 1:5], cw[1], cvs_ap, ALU.mult, ALU.add),
                        (rUE, rCONST, scr.res[cv]), (scr.res[cv],))
                    dve(lambda e, cvs_ap=cvs_ap, cw=cw, j=j: e.scalar_tensor_tensor(
                        cvs_ap, UE[:, j, :, 2:6], cw[2], cvs_ap, ALU.mult, ALU.add),
                        (rUE, rCONST, scr.res[cv]), (scr.res[cv],))
                    cvs.append(cv)
                sB, vB = load_slab(w_in_v[:, :, 2048 + 256 * jp:2048 + 256 * jp + 256], NCH, 256)
                for jj, j in enumerate(js):
                    g = mm_group([(vB[:, k, 128 * jj:128 * jj + 128], A_rhs_main(k)) for k in range(NCH)], TM,
                                 tuple(rA) + (rSLOT[sB],))
                    cv_ap = scr.aps[cvs[jj]]
                    dve(lambda e, cv_ap=cv_ap, g=g, j=j: e.tensor_tensor(
                        split2(B_r[:, j, :]), ps2(g, TM), split2(cv_ap[:, HALO:WF]), ALU.mult),
                        (rPS[g], scr.res[cvs[jj]]), (rB[j],))
                    scr.free(cvs[jj])
            act(lambda e: e.activation(STU[:, :, 2:18].rearrange("p c (s t) -> p c s t", t=2), UE[:, :, :, 4:6],
                                       AF.Copy), (rUE,), (rSTU,))
            for half in range(2):
                g = next_pg()

                def fn(e, half=half, g=g):
                    last = None
                    for c in range(8):
                        last = e.transpose(PS[g][0:18, c * 128:(c + 1) * 128], STU[:, 8 * half + c, :],
                                           IDENT[:, :])
                    return last
                P.emit("pe", fn, (rSTU, rID), (rPS[g],))
                copy_any(XS(0)[0:18, 1024 * half:1024 * (half + 1)], PS[g][0:18, :], (rPS[g],), (rXS[0],))
            P.dma("sp", lambda e, pi=pi: e.dma_start(out=conv_o[pi, :, :], in_=XS(0)[0:18, :]),
                  reads=(rXS[0],), is_output=True)
            raw_dma(pi, 1)

            vWP = WPOOL[:]

            def emit_s3(grp):
                pls = [(grp % 2) * 2, (grp % 2) * 2 + 1]
                for ee in range(2):
                    c = 2 * grp + ee
                    klist = []
                    for kc in range(2):
                        pl_r = RP[:, pls[kc], :]
                        klist.append((vWP[:, 2 * grp + kc, 128 * ee:128 * ee + 128],
                                      (lambda t, pl_r=pl_r: pl_r[:, TM * t:TM * (t + 1)])))
                    g = mm_group(klist, TM, (rP[pls[0]], rP[pls[1]], rWP))
                    act(lambda e, g=g, c=c: e.activation(split2(MIX_r(c)), ps2(g, TM), AF.Copy,
                                                         scale=gcol(G_PS + c)),
                        (rPS[g], rCONST), (rM[c],))
            for jp in range(4):
                grp = jp
                w = POOL_W[grp]
                nstep = {2: 1, 4: 2, 8: 3, 16: 4}[w]
                sV, vV = load_slab(w_in_v[:, :, 6144 + 256 * jp:6144 + 256 * jp + 256], NCH, 256)
                pls = []
                for jj in range(2):
                    j = 2 * jp + jj
                    g = mm_group([(vV[:, k, 128 * jj:128 * jj + 128], A_rhs_full(k)) for k in range(NCH)], TF,
                                 tuple(rA) + (rSLOT[sV],))
                    vv = scr.alloc()
                    vv_ap = scr.aps[vv]
                    act(lambda e, vv_ap=vv_ap, g=g: e.activation(split2(vv_ap[:, 0:WF]), ps2(g, TF), AF.Copy),
                        (rPS[g],), (scr.res[vv],))
                    act(lambda e, vv_ap=vv_ap, j=j: e.activation(
                        VE[:, j, :, 15:19], vv_ap[:, HALO + NPROMPT:WF].rearrange("p (s t) -> p s t", t=4), AF.Copy),
                        (scr.res[vv],), (rVE,))
                    act(lambda e, vv_ap=vv_ap, j=j: e.activation(
                        STV[:, j, 0:15], vv_ap[:, HALO + NPROMPT - 15:HALO + NPROMPT], AF.Copy),
                        (scr.res[vv],), (rSTV,))
                    PE_ = HALO + NPROMPT
                    cur = vv
                    tmps = []
                    for st in range(nstep):
                        sh = 1 << st
                        lo = (1 << (st + 1)) - 1
                        nx = scr.alloc()
                        dve(lambda e, a=scr.aps[cur], o=scr.aps[nx], sh=sh, lo=lo: e.tensor_tensor(
                            o[:, lo:PE_], a[:, lo:PE_], a[:, lo - sh:PE_ - sh], ALU.add),
                            (scr.res[cur],), (scr.res[nx],))
                        if cur != vv:
                            scr.free(cur)
                        cur = nx
                    pl = (jp % 2) * 2 + jj
                    pl_r = RP[:, pl, :]
                    dve(lambda e, s_ap=scr.aps[cur], vv_ap=vv_ap, pl_r=pl_r, w=w: e.scalar_tensor_tensor(
                        pl_r[:, 0:NPROMPT], s_ap[:, HALO:PE_], 1.0 / w, vv_ap[:, HALO:PE_], ALU.mult, ALU.subtract),
                        (scr.res[cur], scr.res[vv]), (rP[pl],))
                    dve(lambda e, s_ap=scr.aps[cur], grp=grp: e.tensor_tensor(
                        T16[:, :], s_ap[:, HALO:HALO + 16], ICNT[:, 16 * grp:16 * grp + 16], ALU.mult),
                        (scr.res[cur], rICNT), (rT16,))
                    dve(lambda e, vv_ap=vv_ap, pl_r=pl_r: e.tensor_tensor(
                        pl_r[:, 0:16], T16[:, :], vv_ap[:, HALO:HALO + 16], ALU.subtract),
                        (rT16, scr.res[vv], rP[pl]), (rP[pl],))
                    scr.free(cur)
                    curs = None
                    for st in range(nstep):
                        sh = 1 << st
                        lo = (1 << (st + 1)) - 1
                        src_ap = VE[:, j, :, :] if curs is None else SS[:, curs, :, :]
                        src_res = rVE if curs is None else rSS[curs]
                        nxs = 0 if curs is None else 1 - curs
                        dve(lambda e, a=src_ap, o=SS[:, nxs, :, :], sh=sh, lo=lo: e.tensor_tensor(
                            o[:, :, lo:19], a[:, :, lo:19], a[:, :, lo - sh:19 - sh], ALU.add),
                            (src_res,), (rSS[nxs],))
                        curs = nxs
                    dve(lambda e, curs=curs, pl_r=pl_r, w=w, j=j: e.scalar_tensor_tensor(
                        pl_r[:, NPROMPT:WM].rearrange("p (s t) -> p s t", t=4), SS[:, curs, :, 15:19], 1.0 / w,
                        VE[:, j, :, 15:19], ALU.mult, ALU.subtract),
                        (rSS[curs], rVE, rP[pl]), (rP[pl],))
                    scr.free(vv)
                    pls.append(pl)
                if jp >= 1:
                    emit_s3(jp - 1)
                load_raw_tile(pi, jp)
            act(lambda e: e.activation(STV[:, :, 15:47].rearrange("p c (s t) -> p c s t", t=4), VE[:, :, :, 15:19],
                                       AF.Copy), (rVE,), (rSTV,))

            def B_rhs(k):
                return lambda t: B_r[:, k, TM * t:TM * (t + 1)]

            def C_rhs(k):
                return lambda t: C_r[:, k, TM * t:TM * (t + 1)]

            def MIX_rhs(k):
                return lambda t: MIX_r(k)[:, TM * t:TM * (t + 1)]

            for jp in range(NCH // 2):
                js = (2 * jp, 2 * jp + 1)
                sG, vG = load_slab(w_in_v[:, :, 7168 + 256 * jp:7168 + 256 * jp + 256], NCH, 256)
                sgc = []
                for jj, j in enumerate(js):
                    g = mm_group([(vG[:, k, 128 * jj:128 * jj + 128], A_rhs_main(k)) for k in range(NCH)], TM,
                                 tuple(rA) + (rSLOT[sG],))
                    t = scr.alloc()
                    act(lambda e, t=t, g=g: e.activation(split2(scr.aps[t][:, 0:WM]), ps2(g, TM), AF.Sigmoid),
                        (rPS[g],), (scr.res[t],))
                    sgc.append(t)
                sY, vY = load_slab(w_brc_v[:, :, 256 * jp:256 * jp + 256], NCH, 256)
                m1 = []
                for jj, j in enumerate(js):
                    g = mm_group([(vY[:, k, 128 * jj:128 * jj + 128], B_rhs(k)) for k in range(NCH)], TM,
                                 tuple(rB) + (rSLOT[sY],))
                    t = scr.alloc()
                    dve(lambda e, t=t, g=g, s=sgc[jj]: e.tensor_tensor(
                        split2(scr.aps[t][:, 0:WM]), ps2(g, TM), split2(scr.aps[s][:, 0:WM]), ALU.mult),
                        (rPS[g], scr.res[sgc[jj]]), (scr.res[t],))
                    scr.free(sgc[jj])
                    m1.append(t)
                sG2, vG2 = load_slab(w_in_v[:, :, 9216 + 256 * jp:9216 + 256 * jp + 256], NCH, 256)
                sgp = []
                for jj, j in enumerate(js):
                    g = mm_group([(vG2[:, k, 128 * jj:128 * jj + 128], A_rhs_main(k)) for k in range(NCH)], TM,
                                 tuple(rA) + (rSLOT[sG2],))
                    t = scr.alloc()
                    act(lambda e, t=t, g=g: e.activation(split2(scr.aps[t][:, 0:WM]), ps2(g, TM), AF.Sigmoid),
                        (rPS[g],), (scr.res[t],))
                    sgp.append(t)
                if jp == 0:
                    emit_s3(3)
                    load_raw_tile(pi, 4)
                sY2, vY2 = load_slab(w_brp_v[:, :, 256 * jp:256 * jp + 256], 8, 256)
                for jj, j in enumerate(js):
                    g = mm_group([(vY2[:, k, 128 * jj:128 * jj + 128], MIX_rhs(k)) for k in range(8)], TM,
                                 tuple(rM) + (rSLOT[sY2],))
                    t = scr.alloc()
                    dve(lambda e, t=t, g=g, s=sgp[jj]: e.tensor_tensor(
                        split2(scr.aps[t][:, 0:WM]), ps2(g, TM), split2(scr.aps[s][:, 0:WM]), ALU.mult),
                        (rPS[g], scr.res[sgp[jj]]), (scr.res[t],))
                    scr.free(sgp[jj])
                    dve(lambda e, t=t, m=m1[jj], j=j: e.tensor_tensor(
                        C_r[:, j, :], scr.aps[t][:, 0:WM], scr.aps[m][:, 0:WM], ALU.add),
                        (scr.res[t], scr.res[m1[jj]]), (rC[j],))
                    scr.free(t)
                    scr.free(m1[jj])

            g = next_pg()

            def fn(e, g=g):
                last = None
                for c in range(8):
                    last = e.transpose(PS[g][0:47, c * 128:(c + 1) * 128], STV[:, c, :], IDENT[:, :])
                return last
            P.emit("pe", fn, (rSTV, rID), (rPS[g],))
            copy_any(XS(1)[0:47, 0:D_POOL], PS[g][0:47, :], (rPS[g],), (rXS[1],))
            P.dma("sp", lambda e, pi=pi: e.dma_start(out=pool_op[pi, :, :], in_=XS(1)[0:15, 0:D_POOL]),
                  reads=(rXS[1],), is_output=True)
            for s in range(NSAMP):
                P.dma("sp", lambda e, pi=pi, s=s: e.dma_start(out=pool_os[pi, s, 11:15, :],
                                                              in_=XS(1)[15 + 4 * s:19 + 4 * s, 0:D_POOL]),
                      reads=(rXS[1],), is_output=True)

            nrm = Norm(lambda j: FX[:, j, HALO:WF], rF, WM, G_FFN, lambda j: B_r[:, j, :], rB)
            for jp in range(NCH // 2):
                sO, vO = load_slab(w_o_v[:, :, 256 * jp:256 * jp + 256], NCH, 256)
                for jj in range(2):
                    j = 2 * jp + jj
                    g = mm_group([(vO[:, k, 128 * jj:128 * jj + 128], C_rhs(k)) for k in range(NCH)], TM,
                                 tuple(rC) + (rSLOT[sO],))
                    dve(lambda e, g=g, j=j: e.tensor_tensor(
                        split2(FX[:, j, HALO:WF]), ps2(g, TM), split2(FX[:, j, HALO:WF]), ALU.add),
                        (rPS[g], rF[j]), (rF[j],))
                    act(lambda e, j=j: e.activation(B_r[:, j, :], FX[:, j, HALO:WF], AF.Copy,
                                                    scale=gcol(G_FFN + j)),
                        (rF[j], rCONST), (rB[j],))
                    nrm.add(j)
            if debug and "x1" in debug:
                for j in range(NCH):
                    dbg_dump("x1", FX[:, j, HALO:WF], (rF[j],), idx=(pi, slice(None), j))

            rb6 = nrm.finish_stats()
            rb6_ap = scr.aps[rb6][:, 0:WM]

            def HN_rhs(k):
                return lambda t: B_r[:, k, TM * t:TM * (t + 1)]

            nblk = (NFF + 7) // 8
            def emit_down(fb):
                nf = min(8, NFF - 8 * fb)
                base = (fb % 2) * 8
                for jq in range(4):
                    sD, vD = load_slab(w_down_v[:, 8 * fb:8 * fb + nf, 512 * jq:512 * jq + 512], nf, 512)
                    for jj in range(4):
                        j = 4 * jq + jj
                        g = mm_group([(vD[:, k, 128 * jj:128 * jj + 128], C_rhs(base + k)) for k in range(nf)], TM,
                                     tuple(rC[base:base + nf]) + (rSLOT[sD],))
                        dve(lambda e, g=g, j=j: e.tensor_tensor(
                            split2(FX[:, j, HALO:WF]), ps2(g, TM), split2(FX[:, j, HALO:WF]), ALU.add),
                            (rPS[g], rF[j]), (rF[j],))

            nxt_steps = s0_steps(pi + 1, pi + 2 == NPASS) if pi + 1 < NPASS else []
            pair_no = 0
            for fb in range(nblk):
                nf = min(8, NFF - 8 * fb)
                base = (fb % 2) * 8
                for fp in range(nf // 2):
                    if pair_no % 2 == 1 and nxt_steps:
                        nxt_steps.pop(0)()
                    pair_no += 1
                    col = 1024 * fb + 256 * fp
                    sG, vG = load_slab(w_gate_v[:, :, col:col + 256], NCH, 256)
                    sl = []
                    for ff in range(2):
                        g = mm_group([(vG[:, k, 128 * ff:128 * ff + 128], HN_rhs(k)) for k in range(NCH)], TM,
                                     tuple(rB) + (rSLOT[sG],))
                        t = scr.alloc()
                        t_ap = scr.aps[t][:, 0:WM]
                        dve(lambda e, t_ap=t_ap, g=g, rb6_ap=rb6_ap: e.tensor_tensor(
                            split2(t_ap), ps2(g, TM), split2(rb6_ap), ALU.mult),
                            (rPS[g], scr.res[rb6]), (scr.res[t],))
                        act(lambda e, t_ap=t_ap: e.activation(t_ap, t_ap, AF.Silu), (scr.res[t],), (scr.res[t],))
                        dve(lambda e, t_ap=t_ap, rb6_ap=rb6_ap: e.tensor_tensor(t_ap, t_ap, rb6_ap, ALU.mult),
                            (scr.res[t], scr.res[rb6]), (scr.res[t],))
                        sl.append(t)
                    sU, vU = load_slab(w_up_v[:, :, col:col + 256], NCH, 256)
                    for ff in range(2):
                        slot = base + 2 * fp + ff
                        g = mm_group([(vU[:, k, 128 * ff:128 * ff + 128], HN_rhs(k)) for k in range(NCH)], TM,
                                     tuple(rB) + (rSLOT[sU],))
                        dve(lambda e, g=g, t=sl[ff], slot=slot: e.tensor_tensor(
                            split2(C_r[:, slot, :]), ps2(g, TM), split2(scr.aps[t][:, 0:WM]), ALU.mult),
                            (rPS[g], scr.res[sl[ff]]), (rC[slot],))
                        scr.free(sl[ff])
                if fb >= 1:
                    emit_down(fb - 1)
            emit_down(nblk - 1)
            while nxt_steps:
                nxt_steps.pop(0)()

            if debug and "x2" in debug:
                for j in range(NCH):
                    dbg_dump("x2", FX[:, j, HALO:WF], (rF[j],), idx=(pi, slice(None), j))
            scr.free(rb6)
            if pi == NPASS - 1:
                for tt in range(5):
                    emit_s8_tile(pi, tt)
            else:
                s8_pending[0] = pi

        P.finish()

        with nc.Block() as block:
            @block.tensor
            def _(e):
                for op in P.ops["pe"]:
                    op(e)

            @block.scalar
            def _(e):
                for op in P.ops["act"]:
                    op(e)

            @block.vector
            def _(e):
                for op in P.ops["dve"]:
                    op(e)

            @block.gpsimd
            def _(e):
                for op in P.ops["pool"]:
                    op(e)

            @block.sync
            def _(e):
                for op in P.ops["sp"]:
                    op(e)
    return nc


def _fm(v):
    return np.ascontiguousarray(np.asarray(v, np.float32).reshape(-1, 128).T)


def make_in_maps(x_prompt, x_sample, state_conv, state_pool, norm_mix, w_in, conv_w, w_pool, pool_scale,
                 w_br_conv, w_br_pool, w_o, norm_ffn, w_gate, w_up, w_down, norm_final):
    f = lambda a: np.ascontiguousarray(np.asarray(a, np.float32))
    x_prompt, x_sample = f(x_prompt), f(x_sample)
    state_conv, state_pool = f(state_conv), f(state_pool)
    gains = np.concatenate([_fm(norm_mix[0]), _fm(norm_ffn[0]), _fm(norm_final)] +
                           [_fm(conv_w[0][k]) for k in range(3)] + [_fm(pool_scale[0])], axis=1)
    gains = np.ascontiguousarray(gains, np.float32)
    assert gains.shape == (128, NGAIN)
    ident = np.eye(128, dtype=np.float32)
    shared = {
        "gains": gains, "ident": ident,
        "gfin_bc": np.ascontiguousarray(np.broadcast_to(np.asarray(norm_final, np.float32)[None, :], (128, D))),
        "gmix_bc": np.ascontiguousarray(np.broadcast_to(np.asarray(norm_mix[0], np.float32)[None, :], (128, D))),
        "w_in": f(w_in[0]), "w_pool": f(w_pool[0]), "w_br_conv": f(w_br_conv[0]), "w_br_pool": f(w_br_pool[0]),
        "w_o": f(w_o[0]), "w_gate": f(w_gate[0]), "w_up": f(w_up[0]), "w_down": f(w_down[0]),
    }
    in_maps = []
    for c in range(8):
        xin = np.zeros((NPASS, WF, D), np.float32)
        sconv = np.zeros((NPASS, 2 * NSAMP, D), np.float32)
        spool = np.zeros((NPASS, 15 * NSAMP, D_POOL), np.float32)
        icnt = np.zeros((NPASS, 128, 64), np.float32)
        for p in range(NPASS):
            vs = 2 * c + p
            b, q = vs // 4, vs % 4
            if q > 0:
                xin[p, 0:HALO] = x_prompt[b, q * NPROMPT - HALO:q * NPROMPT]
            xin[p, HALO:HALO + NST] = x_sample[vs * NSAMP:(vs + 1) * NSAMP].reshape(NST, D)
            xin[p, HALO + NST:] = x_prompt[b, q * NPROMPT:(q + 1) * NPROMPT]
            sconv[p] = state_conv[0, vs * NSAMP:(vs + 1) * NSAMP].reshape(2 * NSAMP, D)
            spool[p] = state_pool[0, vs * NSAMP:(vs + 1) * NSAMP].reshape(15 * NSAMP, D_POOL)
            pos = q * NPROMPT + np.arange(16)
            for g, w in enumerate(POOL_W):
                icnt[p, :, 16 * g:16 * g + 16] = (1.0 / np.minimum(pos + 1, w)).astype(np.float32)[None, :]
        m = dict(shared)
        m.update({"xin": xin, "sconv": sconv, "spool": spool, "icnt": icnt})
        in_maps.append(m)
    return in_maps


def assemble(results):
    y_prompt = np.zeros((4, 2048, D), np.float32)
    y_sample = np.zeros((128, 4, D), np.float32)
    ncp = np.zeros((1, 4, 2, D), np.float32)
    npp = np.zeros((1, 4, 15, D_POOL), np.float32)
    ncs = np.zeros((1, 128, 2, D), np.float32)
    nps = np.zeros((1, 128, 15, D_POOL), np.float32)
    for c in range(8):
        r = results[c]
        for p in range(NPASS):
            vs = 2 * c + p
            b, q = vs // 4, vs % 4
            y_prompt[b, q * NPROMPT:(q + 1) * NPROMPT] = r["y_p"][p]
            y_sample[vs * NSAMP:(vs + 1) * NSAMP] = r["y_s"][p].reshape(NSAMP, 4, D)
            ncs[0, vs * NSAMP:(vs + 1) * NSAMP] = r["conv_o"][p][2:].reshape(NSAMP, 2, D)
            nps[0, vs * NSAMP:(vs + 1) * NSAMP] = r["pool_os"][p]
            if q == 3:
                ncp[0, b] = r["conv_o"][p][0:2]
                npp[0, b] = r["pool_op"][p]
    return (y_prompt, y_sample, ncp, npp, ncs, nps)


def kernel(x_prompt, x_sample, state_conv, state_pool, norm_mix, w_in, conv_w, w_pool, pool_scale,
           w_br_conv, w_br_pool, w_o, norm_ffn, w_gate, w_up, w_down, norm_final):
    in_maps = make_in_maps(x_prompt, x_sample, state_conv, state_pool, norm_mix, w_in, conv_w, w_pool,
                           pool_scale, w_br_conv, w_br_pool, w_o, norm_ffn, w_gate, w_up, w_down, norm_final)
    nc = build_program()
    res = run_bass_kernel_spmd(nc, in_maps, core_ids=list(range(8)))
    return assemble(res.results)
```
